# Optimizing a Trainium2 kernel written in Bass

```python
import math
import jax
import jax.numpy as jnp
from jax import lax
import numpy as np

D_MODEL = 2048
BATCH = 16
SEQ = 256
DEPTH = 2
DEC_BATCH = 8
DEC_SEQ = 1024
PAST_LEN = 256

GRID_W = 64
EPS = 1e-6
N_BRANCH = 4
BRANCH_W = 1024
N_MOD = 6

D_RNN = 1024
LRU_BLOCKS = 8
LRU_BS = D_RNN // LRU_BLOCKS
LRU_CONV = 4
LRU_C = 8.0

RET_HEADS = 4
RET_DK = 128
RET_DV = 256
RET_CHUNK = 64

SSD_HEADS = 16
SSD_P = 64
SSD_N = 128
SSD_GROUPS = 2
SSD_CONV = 4
SSD_CHUNK = 64
D_SSD = SSD_HEADS * SSD_P
SSD_CONV_CH = D_SSD + 2 * SSD_GROUPS * SSD_N

NA_HEADS = 8
NA_HD = 128
NA_W = NA_HEADS * NA_HD
NA_WR = 8
NA_WC = 16
ROPE_BASE = 10000.0
Q_BLOCK = 128

D_FF = 5632
FFN_CONV = 3

IN_SIZES = (D_RNN, D_RNN,
            RET_HEADS * RET_DK, RET_HEADS * RET_DK, RET_HEADS * RET_DV, RET_HEADS * RET_DV,
            D_SSD, SSD_CONV_CH, 2 * SSD_HEADS,
            NA_W, NA_W, NA_W)
D_IN = sum(IN_SIZES)
IN_OFFSETS = tuple(int(s) for s in np.cumsum(IN_SIZES)[:-1])

kernel_name = "hybrid_diffusion_prefix_trunk_step"


def rmsnorm(x, g):
    xf = x.astype(jnp.float32)
    y = xf * lax.rsqrt(jnp.mean(xf * xf, axis=-1, keepdims=True) + EPS)
    return (y * g).astype(x.dtype)


def _rev(t):
    return jnp.flip(t, axis=1)


def dwconv(x, w, b, pad_left):
    K = w.shape[0]
    L = x.shape[1]
    xp = jnp.pad(x, ((0, 0), (pad_left, K - 1 - pad_left), (0, 0)))
    out = b
    for kk in range(K):
        out = out + xp[:, kk:kk + L] * w[kk]
    return out


def chunked_linear_recurrence(q, k, v, log_a, s0, chunk):
    f32 = jnp.float32
    b, L, h, dk = q.shape
    dv = v.shape[-1]
    nc = L // chunk
    qc = q.astype(f32).reshape(b, nc, chunk, h, dk)
    kc = k.astype(f32).reshape(b, nc, chunk, h, dk)
    vc = v.astype(f32).reshape(b, nc, chunk, h, dv)
    cum = jnp.cumsum(log_a.astype(f32).reshape(b, nc, chunk, h), axis=2)
    causal = jnp.tril(jnp.ones((chunk, chunk), dtype=bool))
    seg = cum[:, :, :, None, :] - cum[:, :, None, :, :]
    decay = jnp.exp(jnp.where(causal[None, None, :, :, None], seg, -jnp.inf))
    scores = jnp.einsum('bcthd,bcshd->bctsh', qc, kc) * decay
    y_intra = jnp.einsum('bctsh,bcshv->bcthv', scores, vc)
    tail = jnp.exp(cum[:, :, -1:, :] - cum)
    chunk_states = jnp.einsum('bcshd,bcsh,bcshv->bchdv', kc, tail, vc)
    chunk_decay = jnp.exp(cum[:, :, -1, :])

    def step(S, inp):
        st, dec = inp
        return dec[..., None, None] * S + st, S

    s_final, s_in = lax.scan(step, s0.astype(f32),
                             (chunk_states.swapaxes(0, 1), chunk_decay.swapaxes(0, 1)))
    s_in = s_in.swapaxes(0, 1)
    y_inter = jnp.einsum('bcthd,bchdv->bcthv', qc * jnp.exp(cum)[..., None], s_in)
    return (y_intra + y_inter).reshape(b, L, h, dv), s_final


def _lin_combine(e1, e2):
    a1, b1 = e1
    a2, b2 = e2
    return a1 * a2, a2 * b1 + b2


def rglru_direction(xc, wa, ba, wx, bx, lam, h0, reverse):
    f32 = jnp.float32
    b, L, c = xc.shape
    xb = xc.reshape(b, L, LRU_BLOCKS, LRU_BS)
    r = jax.nn.sigmoid(jnp.einsum('blnd,nde->blne', xb, wa).reshape(b, L, c) + ba)
    i = jax.nn.sigmoid(jnp.einsum('blnd,nde->blne', xb, wx).reshape(b, L, c) + bx)
    log_a = (-LRU_C * r * jax.nn.softplus(-lam)).astype(f32)
    u = jnp.sqrt(-jnp.expm1(2.0 * log_a)) * (i * xc).astype(f32)
    if reverse:
        log_a, u = _rev(log_a), _rev(u)
    a_cum, h = lax.associative_scan(_lin_combine, (jnp.exp(log_a), u), axis=1)
    h = h + a_cum * h0.astype(f32)[:, None, :]
    h_final = h[:, -1]
    if reverse:
        h = _rev(h)
    return h, h_final


def lru_branch(x_in, gate_in, p, h0_f, h0_b):
    xc = dwconv(x_in, p["lru_conv_w"], p["lru_conv_b"], LRU_CONV // 2)
    h_f, s_f = rglru_direction(xc, p["lru_wa"][0], p["lru_ba"][0], p["lru_wx"][0],
                               p["lru_bx"][0], p["lru_lambda"][0], h0_f, False)
    h_b, s_b = rglru_direction(xc, p["lru_wa"][1], p["lru_ba"][1], p["lru_wx"][1],
                               p["lru_bx"][1], p["lru_lambda"][1], h0_b, True)
    y = (h_f + h_b).astype(x_in.dtype) * jax.nn.gelu(gate_in)
    return y, s_f, s_b


def retention_branch(q, k, v, g, p, s0_f, s0_b):
    f32 = jnp.float32
    b, L, _ = q.shape
    q = q.reshape(b, L, RET_HEADS, RET_DK)
    k = k.reshape(b, L, RET_HEADS, RET_DK) * (RET_DK ** -0.5)
    v = v.reshape(b, L, RET_HEADS, RET_DV)
    hh = jnp.arange(RET_HEADS, dtype=f32)
    la_f = jnp.broadcast_to(jnp.log1p(-jnp.exp2(-5.0 - hh)), (b, L, RET_HEADS))
    la_b = jnp.broadcast_to(jnp.log1p(-jnp.exp2(-5.5 - hh)), (b, L, RET_HEADS))
    y_f, s_f = chunked_linear_recurrence(q, k, v, la_f, s0_f, RET_CHUNK)
    y_b, s_b = chunked_linear_recurrence(_rev(q), _rev(k), _rev(v), la_b, s0_b, RET_CHUNK)
    y = y_f + _rev(y_b)
    mu = jnp.mean(y, axis=-1, keepdims=True)
    var = jnp.mean(jnp.square(y - mu), axis=-1, keepdims=True)
    y = (y - mu) * lax.rsqrt(var + EPS) * p["ret_gn_g"].reshape(RET_HEADS, RET_DV)
    y = y.reshape(b, L, RET_HEADS * RET_DV) * jax.nn.silu(g.astype(f32))
    return y.astype(g.dtype), s_f, s_b


def ssd_branch(z, xbc, dt_raw, p, s0_f, s0_b):
    f32 = jnp.float32
    b, L, _ = z.shape
    xbc = jax.nn.silu(dwconv(xbc, p["ssd_conv_w"], p["ssd_conv_b"], SSD_CONV // 2))
    xs, Bm, Cm = jnp.split(xbc, [D_SSD, D_SSD + SSD_GROUPS * SSD_N], axis=-1)
    rep = SSD_HEADS // SSD_GROUPS
    x = xs.reshape(b, L, SSD_HEADS, SSD_P).astype(f32)
    Bh = jnp.repeat(Bm.reshape(b, L, SSD_GROUPS, SSD_N), rep, axis=2)
    Ch = jnp.repeat(Cm.reshape(b, L, SSD_GROUPS, SSD_N), rep, axis=2)
    A = -jnp.exp(p["ssd_a_log"].astype(f32))
    dt = jax.nn.softplus(dt_raw.reshape(b, L, 2, SSD_HEADS).astype(f32)
                         + p["ssd_dt_bias"].astype(f32))
    y_f, s_f = chunked_linear_recurrence(Ch, Bh, x * dt[:, :, 0, :, None],
                                         dt[:, :, 0] * A[0], s0_f, SSD_CHUNK)
    y_b, s_b = chunked_linear_recurrence(_rev(Ch), _rev(Bh), _rev(x * dt[:, :, 1, :, None]),
                                         _rev(dt[:, :, 1] * A[1]), s0_b, SSD_CHUNK)
    y = y_f + _rev(y_b) + x * p["ssd_d"].astype(f32)[:, None]
    y = y.reshape(b, L, D_SSD) * jax.nn.silu(z.astype(f32))
    return rmsnorm(y, p["ssd_norm_g"]).astype(z.dtype), s_f, s_b


def rope_2d(x):
    f32 = jnp.float32
    b, L, h, d = x.shape
    t = jnp.arange(L)
    half = d // 2
    inv = ROPE_BASE ** (-jnp.arange(half // 2, dtype=f32) / (half // 2))

    def rotate(xa, pos):
        ang = pos.astype(f32)[:, None] * inv
        cos = jnp.cos(ang)[None, :, None, :]
        sin = jnp.sin(ang)[None, :, None, :]
        x1, x2 = jnp.split(xa.astype(f32), 2, axis=-1)
        return jnp.concatenate([x1 * cos - x2 * sin, x1 * sin + x2 * cos], axis=-1)

    out = jnp.concatenate([rotate(x[..., :half], t // GRID_W), rotate(x[..., half:], t % GRID_W)], axis=-1)
    return out.astype(x.dtype)


def context_attention(q, k, v):
    b, L, h, d = q.shape
    nb = L // Q_BLOCK
    qb = q.reshape(b, nb, Q_BLOCK, h, d).swapaxes(0, 1)
    scale = d ** -0.5

    def one(qi):
        s = jnp.einsum('bqhd,bkhd->bhqk', qi, k).astype(jnp.float32) * scale
        pr = jax.nn.softmax(s, axis=-1).astype(v.dtype)
        return jnp.einsum('bhqk,bkhd->bqhd', pr, v)

    o = lax.map(one, qb)
    return o.swapaxes(0, 1).reshape(b, L, h, d)


def neighbourhood_attention(q, k, v, k_ctx, v_ctx, rpb):
    b, T, h, d = q.shape
    rows = T // GRID_W
    wr = min(NA_WR, rows)
    nk = wr * GRID_W
    r = jnp.arange(rows)
    row_start = jnp.clip(r - wr // 2, 0, rows - wr)
    row_idx = row_start[:, None] + jnp.arange(wr)[None, :]
    kg = k.reshape(b, rows, GRID_W, h, d)[:, row_idx].reshape(b, rows, nk, h, d)
    vg = v.reshape(b, rows, GRID_W, h, d)[:, row_idx].reshape(b, rows, nk, h, d)
    qg = q.reshape(b, rows, GRID_W, h, d)
    scale = d ** -0.5
    cq = jnp.arange(GRID_W)
    col_start = jnp.clip(cq - NA_WC // 2, 0, GRID_W - NA_WC)
    kc = jnp.tile(jnp.arange(GRID_W), wr)
    valid = (kc[None, :] >= col_start[:, None]) & (kc[None, :] < col_start[:, None] + NA_WC)
    row_off = jnp.repeat(row_idx - r[:, None], GRID_W, axis=1)
    col_off = jnp.clip(kc[None, :] - cq[:, None], 1 - NA_WC, NA_WC - 1)
    bias = rpb[:, row_off[:, None, :] + NA_WR - 1, col_off[None, :, :] + NA_WC - 1]
    s_loc = jnp.einsum('brqhd,brkhd->bhrqk', qg, kg).astype(jnp.float32) * scale + bias[None].astype(jnp.float32)
    s_loc = jnp.where(valid[None, None, None], s_loc, -1e30)
    s_ctx = jnp.einsum('brqhd,bchd->bhrqc', qg, k_ctx).astype(jnp.float32) * scale
    pr = jax.nn.softmax(jnp.concatenate([s_loc, s_ctx], axis=-1), axis=-1).astype(v.dtype)
    o = (jnp.einsum('bhrqk,brkhd->brqhd', pr[..., :nk], vg)
         + jnp.einsum('bhrqc,bchd->brqhd', pr[..., nk:], v_ctx))
    return o.reshape(b, T, h, d)


def mixer(xn, p, init, ctx_kv):
    b, L, _ = xn.shape
    (lru_x, lru_g, ret_q, ret_k, ret_v, ret_g, ssd_z, ssd_xbc, ssd_dt,
     na_q, na_k, na_v) = jnp.split(xn @ p["w_in"], IN_OFFSETS, axis=-1)
    y_lru, lru_f, lru_b = lru_branch(lru_x, lru_g, p, init[0], init[1])
    y_ret, ret_f, ret_b = retention_branch(ret_q, ret_k, ret_v, ret_g, p, init[2], init[3])
    y_ssd, ssd_f, ssd_b = ssd_branch(ssd_z, ssd_xbc, ssd_dt, p, init[4], init[5])
    q = rmsnorm(na_q.reshape(b, L, NA_HEADS, NA_HD), p["na_q_g"])
    k = rmsnorm(na_k.reshape(b, L, NA_HEADS, NA_HD), p["na_k_g"])
    v = na_v.reshape(b, L, NA_HEADS, NA_HD)
    if ctx_kv is None:
        y_na = context_attention(q, k, v)
        kv = (k, v)
    else:
        y_na = neighbourhood_attention(rope_2d(q), rope_2d(k), v, ctx_kv[0], ctx_kv[1], p["na_rpb"])
        kv = ctx_kv
    y_na = y_na.reshape(b, L, NA_W)
    gates = jax.nn.sigmoid(xn @ p["w_gate"] + p["b_gate"]).reshape(b, L, N_BRANCH, D_MODEL)
    branches = jnp.stack([y_lru, y_ret, y_ssd, y_na], axis=2)
    proj = jnp.einsum('blnc,ncd->blnd', branches, p["w_branch"])
    out = jnp.sum(gates * proj, axis=2) @ p["w_out"]
    return out, (kv[0], kv[1], lru_f, lru_b, ret_f, ret_b, ssd_f, ssd_b)


def conv_ffn(xn, p):
    a, v = jnp.split(xn @ p["ffn_w_up"], 2, axis=-1)
    a = dwconv(a, p["ffn_conv_w"], p["ffn_conv_b"], FFN_CONV // 2)
    return (jax.nn.gelu(a) * v) @ p["ffn_w_down"]


def block(x, cond, p, init, ctx_kv):
    sh_a, sc_a, g_a, sh_f, sc_f, g_f = jnp.split(jax.nn.silu(cond) @ p["w_ada"] + p["b_ada"], N_MOD, axis=-1)
    xn = rmsnorm(x, p["norm1_g"]) * (1.0 + sc_a) + sh_a
    y, states = mixer(xn, p, init, ctx_kv)
    x = x + g_a * y
    xn = rmsnorm(x, p["norm2_g"]) * (1.0 + sc_f) + sh_f
    x = x + g_f * conv_ffn(xn, p)
    return x, states


def setup_inputs(seed: int = 0) -> dict:
    key = jax.random.key(seed)
    ks = iter(jax.random.split(key, 64))
    f32 = jnp.float32

    def nrm(shape, scale=1.0):
        return jax.random.normal(next(ks), shape, f32) * scale

    def gain(shape):
        return 1.0 + nrm(shape, 0.05)

    def unif(shape, lo, hi):
        return jax.random.uniform(next(ks), shape, f32, lo, hi)

    D = D_MODEL
    a0 = unif((DEPTH, 2, D_RNN), 0.9, 0.999)
    dt0 = jnp.exp(unif((DEPTH, 2, SSD_HEADS), math.log(1e-3), math.log(1e-1)))
    return {
        "x_prompt": nrm((BATCH, SEQ, D)),
        "x_sample": nrm((DEC_BATCH, DEC_SEQ, D)),
        "cache_na_k": nrm((DEC_BATCH, DEPTH, PAST_LEN, NA_HEADS, NA_HD)),
        "cache_na_v": nrm((DEC_BATCH, DEPTH, PAST_LEN, NA_HEADS, NA_HD)),
        "state_lru_f": nrm((DEC_BATCH, DEPTH, D_RNN), 0.5),
        "state_lru_b": nrm((DEC_BATCH, DEPTH, D_RNN), 0.5),
        "state_ret_f": nrm((DEC_BATCH, DEPTH, RET_HEADS, RET_DK, RET_DV), 0.5),
        "state_ret_b": nrm((DEC_BATCH, DEPTH, RET_HEADS, RET_DK, RET_DV), 0.5),
        "state_ssd_f": nrm((DEC_BATCH, DEPTH, SSD_HEADS, SSD_N, SSD_P), 0.5),
        "state_ssd_b": nrm((DEC_BATCH, DEPTH, SSD_HEADS, SSD_N, SSD_P), 0.5),
        "c": nrm((DEC_BATCH, D)),
        "c_ctx": nrm((D,)),
        "norm1_g": gain((DEPTH, D)),
        "norm2_g": gain((DEPTH, D)),
        "w_ada": nrm((DEPTH, D, N_MOD * D), 0.5 * D ** -0.5),
        "b_ada": nrm((DEPTH, N_MOD * D), 0.02),
        "w_in": nrm((DEPTH, D, D_IN), D ** -0.5),
        "w_gate": nrm((DEPTH, D, N_BRANCH * D), D ** -0.5),
        "b_gate": nrm((DEPTH, N_BRANCH * D), 0.02),
        "w_branch": nrm((DEPTH, N_BRANCH, BRANCH_W, D), BRANCH_W ** -0.5),
        "w_out": nrm((DEPTH, D, D), D ** -0.5),
        "lru_conv_w": nrm((DEPTH, LRU_CONV, D_RNN), LRU_CONV ** -0.5),
        "lru_conv_b": nrm((DEPTH, D_RNN), 0.02),
        "lru_wa": nrm((DEPTH, 2, LRU_BLOCKS, LRU_BS, LRU_BS), LRU_BS ** -0.5),
        "lru_ba": nrm((DEPTH, 2, D_RNN), 0.02),
        "lru_wx": nrm((DEPTH, 2, LRU_BLOCKS, LRU_BS, LRU_BS), LRU_BS ** -0.5),
        "lru_bx": nrm((DEPTH, 2, D_RNN), 0.02),
        "lru_lambda": jnp.log(a0) - jnp.log1p(-a0),
        "ret_gn_g": gain((DEPTH, RET_HEADS * RET_DV)),
        "ssd_conv_w": nrm((DEPTH, SSD_CONV, SSD_CONV_CH), SSD_CONV ** -0.5),
        "ssd_conv_b": nrm((DEPTH, SSD_CONV_CH), 0.02),
        "ssd_a_log": jnp.log(unif((DEPTH, 2, SSD_HEADS), 1.0, 16.0)),
        "ssd_dt_bias": dt0 + jnp.log(-jnp.expm1(-dt0)),
        "ssd_d": gain((DEPTH, SSD_HEADS)),
        "ssd_norm_g": gain((DEPTH, D_SSD)),
        "na_q_g": gain((DEPTH, NA_HD)),
        "na_k_g": gain((DEPTH, NA_HD)),
        "na_rpb": nrm((DEPTH, NA_HEADS, 2 * NA_WR - 1, 2 * NA_WC - 1), 0.1),
        "ffn_w_up": nrm((DEPTH, D, 2 * D_FF), D ** -0.5),
        "ffn_conv_w": nrm((DEPTH, FFN_CONV, D_FF), FFN_CONV ** -0.5),
        "ffn_conv_b": nrm((DEPTH, D_FF), 0.02),
        "ffn_w_down": nrm((DEPTH, D_FF, D), D_FF ** -0.5),
    }


def reference(x_prompt, x_sample, cache_na_k, cache_na_v, state_lru_f, state_lru_b,
              state_ret_f, state_ret_b, state_ssd_f, state_ssd_b, c, c_ctx,
              norm1_g, norm2_g, w_ada, b_ada, w_in, w_gate, b_gate, w_branch, w_out,
              lru_conv_w, lru_conv_b, lru_wa, lru_ba, lru_wx, lru_bx, lru_lambda,
              ret_gn_g, ssd_conv_w, ssd_conv_b, ssd_a_log, ssd_dt_bias, ssd_d, ssd_norm_g,
              na_q_g, na_k_g, na_rpb, ffn_w_up, ffn_conv_w, ffn_conv_b, ffn_w_down):
    f32 = jnp.float32
    layers = [dict(norm1_g=norm1_g[l], norm2_g=norm2_g[l], w_ada=w_ada[l], b_ada=b_ada[l],
                   w_in=w_in[l], w_gate=w_gate[l], b_gate=b_gate[l], w_branch=w_branch[l], w_out=w_out[l],
                   lru_conv_w=lru_conv_w[l], lru_conv_b=lru_conv_b[l], lru_wa=lru_wa[l], lru_ba=lru_ba[l],
                   lru_wx=lru_wx[l], lru_bx=lru_bx[l], lru_lambda=lru_lambda[l], ret_gn_g=ret_gn_g[l],
                   ssd_conv_w=ssd_conv_w[l], ssd_conv_b=ssd_conv_b[l], ssd_a_log=ssd_a_log[l],
                   ssd_dt_bias=ssd_dt_bias[l], ssd_d=ssd_d[l], ssd_norm_g=ssd_norm_g[l],
                   na_q_g=na_q_g[l], na_k_g=na_k_g[l], na_rpb=na_rpb[l], ffn_w_up=ffn_w_up[l],
                   ffn_conv_w=ffn_conv_w[l], ffn_conv_b=ffn_conv_b[l], ffn_w_down=ffn_w_down[l])
              for l in range(DEPTH)]

    xp = x_prompt
    bp = xp.shape[0]
    cond_ctx = c_ctx[None, None, :]
    zero_init = (jnp.zeros((bp, D_RNN), f32), jnp.zeros((bp, D_RNN), f32),
                 jnp.zeros((bp, RET_HEADS, RET_DK, RET_DV), f32), jnp.zeros((bp, RET_HEADS, RET_DK, RET_DV), f32),
                 jnp.zeros((bp, SSD_HEADS, SSD_N, SSD_P), f32), jnp.zeros((bp, SSD_HEADS, SSD_N, SSD_P), f32))
    per_layer = []
    for l in range(DEPTH):
        xp, st = block(xp, cond_ctx, layers[l], zero_init, None)
        per_layer.append(st)
    y_prompt = xp
    new_na_k = jnp.stack([s[0] for s in per_layer], axis=1)
    new_na_v = jnp.stack([s[1] for s in per_layer], axis=1)
    new_lru_f = jnp.stack([s[2] for s in per_layer], axis=1)
    new_lru_b = jnp.stack([s[3] for s in per_layer], axis=1)
    new_ret_f = jnp.stack([s[4] for s in per_layer], axis=1)
    new_ret_b = jnp.stack([s[5] for s in per_layer], axis=1)
    new_ssd_f = jnp.stack([s[6] for s in per_layer], axis=1)
    new_ssd_b = jnp.stack([s[7] for s in per_layer], axis=1)

    xs = x_sample
    cond = c[:, None, :]
    for l in range(DEPTH):
        init = (state_lru_f[:, l], state_lru_b[:, l], state_ret_f[:, l], state_ret_b[:, l],
                state_ssd_f[:, l], state_ssd_b[:, l])
        xs, _ = block(xs, cond, layers[l], init, (cache_na_k[:, l], cache_na_v[:, l]))
    y_sample = xs

    return (y_prompt, y_sample, new_na_k, new_na_v, new_lru_f, new_lru_b,
            new_ret_f, new_ret_b, new_ssd_f, new_ssd_b)
```

```python
import numpy as np
import math
import concourse.bass as bass
import concourse.mybir as mybir
from concourse.bass_utils import run_bass_kernel_spmd

F32 = mybir.dt.float32
BF16 = mybir.dt.bfloat16
I32 = mybir.dt.int32
AF = mybir.ActivationFunctionType
ALU = mybir.AluOpType
AX = mybir.AxisListType

ENGS = ("pe", "act", "dve", "pool", "sp")
SEM_LIMIT = 30000


class Buf:
    __slots__ = ("name", "lw", "rd", "excl")

    def __init__(self, name):
        self.name = name
        self.lw = None
        self.rd = {}
        self.excl = False


class V:
    __slots__ = ("ap", "bufs")

    def __init__(self, ap, bufs):
        self.ap = ap
        self.bufs = bufs

    def __getitem__(self, idx):
        return V(self.ap[idx], self.bufs)

    def bitcast(self, dt):
        return V(self.ap.bitcast(dt), self.bufs)

    def rearrange(self, pat, **kw):
        return V(self.ap.rearrange(pat, **kw), self.bufs)

    def unsqueeze(self, ax):
        return V(self.ap.unsqueeze(ax), self.bufs)

    def to_broadcast(self, shape):
        return V(self.ap.to_broadcast(list(shape)), self.bufs)


class T:
    def __init__(self, h, name, nsub=1):
        self.h = h
        self.name = name
        self.bufs = [Buf("%s.%d" % (name, i)) for i in range(nsub)]

    def __getitem__(self, idx):
        return V(self.h[idx], self.bufs)

    def sub(self, i):
        return V(self.h[:, i], [self.bufs[i]])

    def subs(self, i0, i1):
        return V(self.h[:, i0:i1], self.bufs[i0:i1])


class Op:
    __slots__ = ("eng", "idx", "fn", "deps", "is_dma", "sem", "val", "signal", "waits", "vc", "gidx")


class Prog:
    def __init__(self, nc, n_dma_sems=8):
        self.nc = nc
        self.ops = {e: [] for e in ENGS}
        self.order = []
        self.n_dma_sems = n_dma_sems
        self.dma_count = {e: 0 for e in ENGS}
        self.out_bufs = []
        self.last_real = {}
        self.stack = None
        self.all_bufs = []

    def sb(self, name, shape, dtype, nsub=1):
        h = self.nc.alloc_sbuf_tensor(name, list(shape), dtype)
        t = T(h, name, nsub)
        self.all_bufs.extend(t.bufs)
        return t

    def sb_at(self, name, shape, dtype, offset, nsub=1):
        h = self.nc.alloc_sbuf_tensor_at(name, list(shape), dtype, offset=offset)
        t = T(h, name, nsub)
        self.all_bufs.extend(t.bufs)
        return t

    def ps(self, name, shape, dtype=F32):
        h = self.nc.alloc_psum_tensor(name, list(shape), dtype)
        t = T(h, name, 1)
        for b in t.bufs:
            b.excl = True
        self.all_bufs.extend(t.bufs)
        return t

    def emit(self, eng, fn, reads=(), writes=(), dma=False):
        op = Op()
        op.eng = eng
        op.idx = len(self.ops[eng])
        op.fn = fn
        op.is_dma = dma
        op.signal = False
        op.sem = None
        op.val = None
        op.waits = None
        op.vc = None
        op.gidx = len(self.order)
        deps = {}
        rb = [b for v in reads for b in v.bufs]
        wb = [b for v in writes for b in v.bufs]
        wb = wb + [b for b in rb if b.excl]
        rb = [b for b in rb if not b.excl]
        for b in rb:
            if b.lw is not None:
                deps[id(b.lw)] = b.lw
        for b in wb:
            if b.lw is not None:
                deps[id(b.lw)] = b.lw
            for r in b.rd.values():
                deps[id(r)] = r
        deps.pop(id(op), None)
        op.deps = list(deps.values())
        for b in rb:
            key = ("dma", op.gidx) if dma else eng
            b.rd[key] = op
        for b in wb:
            b.lw = op
            b.rd = {}
        self.ops[eng].append(op)
        self.order.append(op)
        if fn is not None and not dma:
            self.last_real[eng] = op
        return op

    def barrier(self):
        last = list(self.last_real.values())
        dmas = list(getattr(self, "_dma_since_barrier", []))
        for e in ENGS:
            op = self.emit(e, None, (), ())
            op.deps = last + dmas
        self._dma_since_barrier = []

    def _recent_dmas(self):
        return list(getattr(self, "_dma_since_barrier", []))

    def track_dma(self, op):
        if not hasattr(self, "_dma_since_barrier"):
            self._dma_since_barrier = []
        self._dma_since_barrier.append(op)

    @staticmethod
    def _need_wait(op, d):
        if d.fn is None:
            return False
        if d.eng != op.eng:
            return True
        if d.eng in ("pe", "sp"):
            return False
        if op.is_dma:
            return True
        return (op.idx - d.idx) <= 1

    def finalize(self):
        nc = self.nc
        dma_sem_of = {}
        self.dma_sems = {}
        per_sem_count = {}
        for op in self.order:
            if op.is_dma:
                k = self.dma_count[op.eng]
                self.dma_count[op.eng] += 1
                s = (op.eng, k % self.n_dma_sems)
                per_sem_count[s] = per_sem_count.get(s, 0) + 1
                op.sem = ("dma",) + s
                op.val = 16 * per_sem_count[s]
                op.signal = True
        for op in self.order:
            for d in op.deps:
                if d.is_dma:
                    continue
                if self._need_wait(op, d):
                    d.signal = True
        cum = {e: 0 for e in ENGS}
        for e in ENGS:
            for op in self.ops[e]:
                if op.is_dma:
                    continue
                if op.signal:
                    cum[e] += 1
                    op.sem = ("eng", e, (cum[e] - 1) // SEM_LIMIT)
                    op.val = (cum[e] - 1) % SEM_LIMIT + 1
        self.cum = cum
        known = {e: {} for e in ENGS}
        lastsig = {e: None for e in ENGS}
        prev_dma_on_sem = {}
        for op in self.order:
            e = op.eng
            kn = known[e]
            waits = []

            def need(sem, val, vc):
                if kn.get(sem, 0) >= val:
                    return
                waits.append((sem, val))
                if vc is not None:
                    for s2, v2 in vc.items():
                        if kn.get(s2, 0) < v2:
                            kn[s2] = v2
                if kn.get(sem, 0) < val:
                    kn[sem] = val

            for d in op.deps:
                if d.is_dma:
                    need(d.sem, d.val, d.vc)
                else:
                    if not self._need_wait(op, d):
                        continue
                    need(d.sem, d.val, d.vc)
            if op.is_dma:
                p = prev_dma_on_sem.get(op.sem)
                if p is not None:
                    need(p.sem, p.val, p.vc)
                prev_dma_on_sem[op.sem] = op
            op.waits = waits
            if op.is_dma:
                vc = dict(kn)
                vc[op.sem] = op.val
                op.vc = vc
            elif op.signal:
                vc = dict(kn)
                vc[op.sem] = op.val
                op.vc = vc
        return self

    def run_emit(self):
        nc = self.nc
        from contextlib import ExitStack
        sem_handles = {}
        with ExitStack() as st:
            def get_sem(key):
                if key not in sem_handles:
                    nm = "s_" + "_".join(str(x) for x in key)
                    sem_handles[key] = st.enter_context(nc.semaphore(nm))
                return sem_handles[key]

            for op in self.order:
                if op.sem is not None and op.signal:
                    get_sem(op.sem)
            block = st.enter_context(nc.Block())

            def run_engine(ename):
                def body(eng):
                    for op in self.ops[ename]:
                        for (sk, val) in op.waits:
                            eng.wait_ge(get_sem(sk), val)
                        if op.fn is None:
                            continue
                        ins = op.fn(eng)
                        if op.signal:
                            ins.then_inc(get_sem(op.sem), 16 if op.is_dma else 1)
                return body

            block.tensor(run_engine("pe"))
            block.scalar(run_engine("act"))
            block.vector(run_engine("dve"))
            block.gpsimd(run_engine("pool"))
            block.sync(run_engine("sp"))

    def dma(self, q, out, in_, out_dram=False, in_dram=False, **kw):
        reads = [] if in_dram else [in_]
        if out_dram:
            b = Buf("dramout%d" % len(self.out_bufs))
            self.out_bufs.append(b)
            wv = [V(None, [b])]
        else:
            wv = [out]
        oap = out.ap if isinstance(out, V) else out
        iap = in_.ap if isinstance(in_, V) else in_
        op = self.emit(q, lambda e: e.dma_start(out=oap, in_=iap, **kw), reads, wv, dma=True)
        self.track_dma(op)
        return op

    def final_fence(self):
        vs = [V(None, [b]) for b in self.out_bufs]
        self.emit("sp", None, vs, ())

    def mm(self, out, lhsT, rhs, start=True, stop=True, **kw):
        return self.emit("pe", lambda e: e.matmul(out.ap, lhsT.ap, rhs.ap, start=start, stop=stop, **kw),
                         [lhsT, rhs], [out])

    def transpose(self, out, in_, ident):
        return self.emit("pe", lambda e: e.transpose(out.ap, in_.ap, ident.ap), [in_, ident], [out])

    def act(self, out, in_, func, bias=None, scale=None, accum_out=None, eng="act"):
        reads = [in_]
        kw = {}
        if bias is not None:
            if isinstance(bias, V):
                reads.append(bias)
                kw["bias"] = bias.ap
            else:
                kw["bias"] = bias
        if scale is not None:
            if isinstance(scale, V):
                reads.append(scale)
                kw["scale"] = scale.ap
            else:
                kw["scale"] = scale
        writes = [out]
        if accum_out is not None:
            writes.append(accum_out)
            kw["accum_out"] = accum_out.ap
        return self.emit("act", lambda e: e.activation(out.ap, in_.ap, func, **kw), reads, writes)

    def tt(self, eng, out, in0, in1, op):
        return self.emit(eng, lambda e: e.tensor_tensor(out.ap, in0.ap, in1.ap, op), [in0, in1], [out])

    def ts(self, eng, out, in0, s1, s2, op0, op1=None):
        reads = [in0]
        a1 = s1
        a2 = s2
        if isinstance(s1, V):
            reads.append(s1)
            a1 = s1.ap
        if isinstance(s2, V):
            reads.append(s2)
            a2 = s2.ap
        if op1 is None:
            return self.emit(eng, lambda e: e.tensor_scalar(out.ap, in0.ap, a1, None, op0), reads, [out])
        return self.emit(eng, lambda e: e.tensor_scalar(out.ap, in0.ap, a1, a2, op0, op1), reads, [out])

    def stt(self, out, in0, scalar, in1, op0, op1, eng="dve"):
        reads = [in0, in1]
        a = scalar
        if isinstance(scalar, V):
            reads.append(scalar)
            a = scalar.ap
        return self.emit(eng, lambda e: e.scalar_tensor_tensor(out.ap, in0.ap, a, in1.ap, op0, op1), reads, [out])

    def copy(self, eng, out, in_):
        if eng == "act":
            return self.emit("act", lambda e: e.copy(out.ap, in_.ap), [in_], [out])
        return self.emit(eng, lambda e: e.tensor_copy(out.ap, in_.ap), [in_], [out])

    def memset(self, eng, out, val):
        return self.emit(eng, lambda e: e.memset(out.ap, val), [], [out])

    def scan(self, out, d0, d1, init, op0=None, op1=None):
        reads = [d0, d1]
        a = init
        if isinstance(init, V):
            reads.append(init)
            a = init.ap
        return self.emit("dve", lambda e: e.tensor_tensor_scan(out.ap, d0.ap, d1.ap, a, op0 or ALU.mult, op1 or ALU.add),
                         reads, [out])

    def recip(self, out, in_):
        return self.emit("dve", lambda e: e.reciprocal(out.ap, in_.ap), [in_], [out])

D = 2048
NCH = 16
EPS = 1e-6
DFF = 5632
NFF = 44
D_IN = 10784
OFF = dict(lru_x=0, lru_g=1024, ret_q=2048, ret_k=2560, ret_v=3072, ret_g=4096, ssd_z=5120,
           ssd_xbc=6144, ssd_dt=7680, na_q=7712, na_k=8736, na_v=9760)

PV_FIELDS = [("n1g", 16), ("n2g", 16), ("bgate", 64), ("lru_cw", 32), ("lru_cb", 8), ("lru_ba", 16),
             ("lru_bx", 16), ("lru_lam", 16), ("ret_gn", 8), ("ssd_cw", 48), ("ssd_cb", 12), ("ssd_ng", 8),
             ("ssd_dd", 8), ("ffn_cw", 132), ("ffn_cb", 44), ("ssd_dtb", 1), ("ssd_alog", 1)]
PV_OFF = {}
_o = 0
for _n, _w in PV_FIELDS:
    PV_OFF[_n] = (_o, _w)
    _o += _w
NPV = _o

ARENA_WORDS = 53000


class Ctx:
    pass


def build_program(stage=99, substage=9, branches=(0, 1, 2, 3)):
    nc = bass.Bass("TRN2", target_bir_lowering=False)
    P = Prog(nc)
    K = Ctx()
    K.P, K.nc, K.stage = P, nc, stage
    K.substage = substage
    K.branches = branches
    di = {}
    K.di = di

    def din(name, shape, dt=F32):
        di[name] = nc.dram_tensor(name, list(shape), dt, kind="ExternalInput").ap()

    def dout(name, shape):
        di[name] = nc.dram_tensor(name, list(shape), F32, kind="ExternalOutput").ap()

    din("xp", [512, D]); din("xs", [1024, D]); din("condT", [128, 16, 2])
    din("w_ada", [2, D, 6 * D]); din("b_adaT", [2, 128, 96]); din("pv", [2, 128, NPV])
    din("w_in", [2, D, D_IN]); din("w_gate", [2, D, 4 * D]); din("w_branch", [2, 4, 1024, D])
    din("w_out", [2, D, D]); din("ffn_w_up", [2, D, 2 * DFF]); din("ffn_w_down", [2, DFF, D])
    din("lru_wa", [2, 2, 8, 128, 128]); din("lru_wx", [2, 2, 8, 128, 128])
    din("ident", [128, 128]); din("lru_h0", [128, 32])
    din("nag", [2, 128, 2, 128]); din("cache_k", [2, 256, 1024]); din("cache_v", [2, 256, 1024])
    din("ret_s0", [2, 2, 4, 128, 256]); din("ssd_s0", [2, 2, 16, 128, 64])
    din("ret_mask", [4, 128, 1920]); din("ret_rows", [128, 8, 1024]); din("ret_tail", [128, 16])
    din("rpbp", [2, 8, 15, 128]); din("na_colmask", [128, 64]); din("rope_cs", [128, 8, 2, 2, 32])
    din("ssd_causal", [128, 2, 896])
    dout("yp", [512, D]); dout("ys", [1024, D])
    dout("nk", [2, 2, 256, 1024]); dout("nv", [2, 2, 256, 1024])
    dout("nlru", [128, 64])
    dout("nret", [2, 2, 2, 4, 128, 256]); dout("nssd", [2, 2, 2, 16, 128, 64])

    arena = nc.alloc_sbuf_tensor("arena", [128, ARENA_WORDS], F32)
    K.arena = arena

    def carve(name, off, words, dtype=F32, shape=None, nsub=1):
        ap = arena[:, off:off + words]
        if dtype == BF16:
            ap = ap.bitcast(BF16)
        if shape is not None:
            if len(shape) == 2:
                ap = ap.rearrange("p (a b) -> p a b", b=shape[1])
            elif len(shape) == 3:
                ap = ap.rearrange("p (a b c) -> p a b c", b=shape[1], c=shape[2])
        return T(ap, name, nsub)
    K.carve = carve

    K.XT = carve("XT", 0, 16384, F32, (16, 1024), nsub=16)
    K.XN = carve("XN", 16384, 8192, BF16, (16, 1024), nsub=16)
    K.WS2 = [carve("WS0", 24576, 3072, BF16), carve("WS1", 27648, 3072, BF16)]
    K.WS = K.WS2
    K.wi = 0
    K.ps_hi = 4
    A0 = 30720
    K.A0 = A0
    K.MG = carve("MG", A0, 8192, BF16, (16, 1024), nsub=16)
    K.YB = carve("YB", A0 + 8192, 4096, BF16, (8, 1024), nsub=8)
    K.HH = carve("HH", A0, 11264, BF16, (44, 512), nsub=44)
    K.HH.bufs = (K.MG.bufs + K.YB.bufs) * 2
    K.HH.bufs = [Buf("HH.%d" % i) for i in range(44)]
    C0 = 43008
    K.ident_f = carve("ident_f", C0, 128)
    K.ident_b = carve("ident_b", C0 + 128, 64, BF16)
    K.ones_f = carve("ones_f", C0 + 192, 128)
    K.ones_b = carve("ones_b", C0 + 320, 64, BF16)
    K.adaT = [carve("adaT%d" % l, C0 + 384 + l * 192, 192, F32, (96, 2)) for l in range(2)]
    K.mod = carve("mod", C0 + 768, 96, F32, (6, 16))
    K.pv = carve("pv", C0 + 864, NPV)
    K.cs = carve("cs", C0 + 864 + NPV, 16, BF16, (16, 2))
    K.stT = carve("stT", C0 + 880 + NPV, 64)
    K.lru_c1 = carve("lru_c1", C0 + 944 + NPV, 32)
    K.lru_h0 = carve("lru_h0", C0 + 976 + NPV, 32)
    assert C0 + 1008 + NPV <= 45056, (C0 + 1008 + NPV)
    K.B0 = 45056
    K.BW = ARENA_WORDS - K.B0
    K.WS3 = K.WS2 + [carve("WS2", K.B0 + 4200, 3072, BF16)]
    K.WS4 = K.WS3 + [carve("WS3", K.B0 + 1100, 3072, BF16)]
    K.PS = [P.ps("ps%d" % i, [128, 512]) for i in range(8)]
    K.psi = 0

    setup(K)
    if stage >= 1:
        run_group(K, "p")
    if stage >= 2:
        run_group(K, "s")
    P.dma("sp", di["nlru"], K.stT[:], out_dram=True)
    P.final_fence()
    P.finalize()
    P.run_emit()
    return nc


def next_ps(K, lo=0, hi=None):
    if hi is None:
        hi = K.ps_hi
    i = lo + (K.psi % (hi - lo))
    K.psi += 1
    return K.PS[i]


def pvv(K, name, j=None):
    o, w = PV_OFF[name]
    if j is None:
        return K.pv[:, o:o + w]
    return K.pv[:, o + j:o + j + 1]


def wview(slot, kc, ncols):
    ap = slot.h[:, 0:kc * ncols].rearrange("p (k n) -> p k n", n=ncols)
    return V(ap, slot.bufs)


def stream_w(K, src2d, nrows, ncols):
    P = K.P
    kc = nrows // 128
    assert kc * ncols <= 6144
    slot = K.WS[K.wi % len(K.WS)]
    K.wi += 1
    v = wview(slot, kc, ncols)
    P.dma("pool", v, src2d.rearrange("(k p) n -> p k n", p=128), in_dram=True)
    return v


def setup(K):
    P, di = K.P, K.di
    P.dma("sp", K.ident_f[:], di["ident"], in_dram=True)
    P.copy("dve", K.ident_b[:], K.ident_f[:])
    P.memset("dve", K.ones_f[:], 1.0)
    P.memset("dve", K.ones_b[:], 1.0)
    P.memset("dve", K.stT[:], 0.0)
    P.dma("sp", K.lru_h0[:], di["lru_h0"], in_dram=True)
    ctmp = K.carve("ctmp", K.B0, 32, F32, (16, 2))
    btmp = K.carve("btmp", K.B0 + 32, 96)
    P.dma("sp", ctmp[:], di["condT"], in_dram=True)
    P.act(K.cs[:], ctmp[:], AF.Silu)
    import os
    for l in range(2):
        if os.environ.get('NO_ADA') == '1':
            break
        psA = K.PS[4 + l]
        for blk in range(32):
            w = stream_w(K, di["w_ada"][l, :, blk * 384:(blk + 1) * 384], D, 384)
            for m in range(3):
                ch = blk * 3 + m
                for kc in range(16):
                    P.mm(psA[:, ch * 2:ch * 2 + 2], w[:, kc, m * 128:(m + 1) * 128], K.cs[:, kc, :],
                         start=(kc == 0), stop=(kc == 15))
        P.dma("sp", btmp[:], di["b_adaT"][l], in_dram=True)
        P.tt("dve", K.adaT[l][:], psA[:, 0:192].rearrange("p (a b) -> p a b", b=2),
             btmp[:].unsqueeze(2).to_broadcast([128, 96, 2]), ALU.add)
    P.barrier()


def layer_setup(K, l, g):
    P, di = K.P, K.di
    gi = 0 if g == "p" else 1
    P.dma("sp", K.pv[:], di["pv"][l], in_dram=True)
    ada = K.adaT[l]
    P.ts("dve", K.mod[:, 0, :], ada[:, 16:32, gi], 1.0, None, ALU.add)
    P.tt("dve", K.mod[:, 0, :], K.mod[:, 0, :], pvv(K, "n1g"), ALU.mult)
    P.copy("dve", K.mod[:, 1, :], ada[:, 0:16, gi])
    P.copy("dve", K.mod[:, 2, :], ada[:, 32:48, gi])
    P.ts("dve", K.mod[:, 3, :], ada[:, 64:80, gi], 1.0, None, ALU.add)
    P.tt("dve", K.mod[:, 3, :], K.mod[:, 3, :], pvv(K, "n2g"), ALU.mult)
    P.copy("dve", K.mod[:, 4, :], ada[:, 48:64, gi])
    P.copy("dve", K.mod[:, 5, :], ada[:, 80:96, gi])


def rms_modulate(K, T_, arow, brow, scr_off):
    P = K.P
    sq = [K.carve("sq%d" % i, scr_off + i * 512, 256, BF16) for i in range(2)]
    rstd = K.carve("rstd", scr_off + 1024, 512)
    tmp = [K.carve("nt%d" % i, scr_off + 1536 + i * 512, 512) for i in range(2)]
    for tt in range(T_ // 512):
        sl = slice(tt * 512, (tt + 1) * 512)
        pst = K.PS[6]
        for c in range(16):
            s = sq[c % 2]
            P.act(s[:], K.XT.sub(c)[:, sl], AF.Square)
            P.mm(pst[:], K.ones_b[:], s[:], start=(c == 0), stop=(c == 15))
        P.act(rstd[:], pst[:], AF.Ln, scale=1.0 / D, bias=EPS)
        P.act(rstd[:], rstd[:], AF.Exp, scale=-0.5)
        for c in range(16):
            t = tmp[c % 2]
            P.tt("dve", t[:], K.XT.sub(c)[:, sl], rstd[:], ALU.mult)
            P.act(K.XN.sub(c)[:, sl], t[:], AF.Identity, scale=K.mod[:, arow, c:c + 1], bias=K.mod[:, brow, c:c + 1])


def split3(K, src, parts, tmp):
    P = K.P
    P.copy("dve", parts[0], src)
    P.tt("dve", tmp, src, parts[0], ALU.subtract)
    P.copy("dve", parts[1], tmp)
    P.tt("dve", tmp, tmp, parts[1], ALU.subtract)
    P.copy("dve", parts[2], tmp)


def load_x(K, g, T_):
    P, di = K.P, K.di
    src = di["xp"] if g == "p" else di["xs"]
    stg = K.carve("xstg", K.B0, 2048)
    tmp = K.carve("xtmp", K.B0 + 2048, 2048)
    parts = [K.carve("xpt%d" % i, K.B0 + 4096 + i * 1024, 1024, BF16) for i in range(3)]
    import os
    for tc_ in range(int(os.environ.get('NCHUNK', T_ // 128))):
        P.dma("sp", stg[:], src[tc_ * 128:(tc_ + 1) * 128, :], in_dram=True)
        LX = int(os.environ.get('LX', 9))
        if LX >= 2:
            split3(K, stg[:], [p_[:] for p_ in parts], tmp[:])
        for c4 in range(4):
            if LX < 3:
                break
            ps = next_ps(K)
            for q in range(4):
                c = c4 * 4 + q
                for i in range(3):
                    P.mm(ps[:, q * 128:(q + 1) * 128], parts[i][:, c * 128:(c + 1) * 128], K.ident_b[:],
                         start=(i == 0), stop=(i == 2))
            if LX >= 4:
                P.copy("act" if c4 % 2 else "dve", K.XT.subs(c4 * 4, c4 * 4 + 4)[:, :, tc_ * 128:(tc_ + 1) * 128],
                       ps[:].rearrange("p (a b) -> p a b", b=128))


def store_x(K, g, T_):
    P, di = K.P, K.di
    dst = di["yp"] if g == "p" else di["ys"]
    stg = [K.carve("ystg%d" % i, K.B0 + i * 2048, 2048) for i in range(2)]
    tmp = K.carve("ytmp", K.B0 + 4096, 1024)
    parts = [K.carve("ypt%d" % i, K.B0 + 5120 + i * 512, 512, BF16) for i in range(3)]
    import os
    for tc_ in range(int(os.environ.get('NCHUNK', T_ // 128))):
        s = stg[tc_ % 2]
        tsl = slice(tc_ * 128, (tc_ + 1) * 128)
        for c4 in range(4):
            ps = next_ps(K)
            for q in range(4):
                c = c4 * 4 + q
                pp = [p_[:, q * 128:(q + 1) * 128] for p_ in parts]
                split3(K, K.XT.sub(c)[:, tsl], pp, tmp[:, q * 128:(q + 1) * 128])
                for i in range(3):
                    P.mm(ps[:, q * 128:(q + 1) * 128], pp[i], K.ident_b[:], start=(i == 0), stop=(i == 2))
            P.copy("act" if c4 % 2 else "dve", s[:, c4 * 512:(c4 + 1) * 512], ps[:])
        P.dma("sp", dst[tsl, :], s[:], out_dram=True)


def merge_branch(K, l, n, T_, first, scr_off):
    P, di = K.P, K.di
    sg = [K.carve("sg%d" % i, scr_off + i * 512, 512) for i in range(2)]
    NT = T_ // 512
    k = 0
    for jb in range(8):
        wg = stream_w(K, di["w_gate"][l, :, n * D + jb * 256: n * D + (jb + 1) * 256], D, 256)
        wb = stream_w(K, di["w_branch"][l, n, :, jb * 256:(jb + 1) * 256], 1024, 256)
        for jj in range(2):
            j = jb * 2 + jj
            for tt in range(NT):
                sl = slice(tt * 512, (tt + 1) * 512)
                pg = next_ps(K)
                pp = next_ps(K)
                for kc in range(16):
                    P.mm(pg[:], wg[:, kc, jj * 128:(jj + 1) * 128], K.XN.sub(kc)[:, sl], start=(kc == 0), stop=(kc == 15))
                for kc in range(8):
                    P.mm(pp[:], wb[:, kc, jj * 128:(jj + 1) * 128], K.YB.sub(kc)[:, sl], start=(kc == 0), stop=(kc == 7))
                s = sg[k % 2]
                k += 1
                P.act(s[:], pg[:], AF.Sigmoid, bias=pvv(K, "bgate", n * 16 + j))
                if first:
                    P.tt("dve", K.MG.sub(j)[:, sl], s[:], pp[:], ALU.mult)
                else:
                    P.tt("dve", s[:], s[:], pp[:], ALU.mult)
                    P.tt("dve", K.MG.sub(j)[:, sl], K.MG.sub(j)[:, sl], s[:], ALU.add)


def out_proj(K, l, T_):
    P, di = K.P, K.di
    NT = T_ // 512
    for jb in range(8):
        w = stream_w(K, di["w_out"][l, :, jb * 256:(jb + 1) * 256], D, 256)
        for jj in range(2):
            j = jb * 2 + jj
            for tt in range(NT):
                sl = slice(tt * 512, (tt + 1) * 512)
                po = next_ps(K)
                for kc in range(16):
                    P.mm(po[:], w[:, kc, jj * 128:(jj + 1) * 128], K.MG.sub(kc)[:, sl], start=(kc == 0), stop=(kc == 15))
                P.stt(K.XT.sub(j)[:, sl], po[:], K.mod[:, 2, j:j + 1], K.XT.sub(j)[:, sl], ALU.mult, ALU.add)


def ffn(K, l, g, T_, scr_off):
    P, di = K.P, K.di
    assert g == "p" and T_ == 512
    nseg, L = 2, 256
    apad = [K.carve("apad%d" % i, scr_off + i * 516, nseg * (L + 2), F32, (nseg, L + 2)) for i in range(2)]
    acc = [K.carve("facc%d" % i, scr_off + 1032 + i * 512, 512, F32, (nseg, L)) for i in range(2)]
    sl = slice(0, 512)
    r3 = lambda v: v.rearrange("p (s l) -> p s l", s=nseg)
    for mp in range(NFF // 2):
        wa = stream_w(K, di["ffn_w_up"][l, :, mp * 256:(mp + 1) * 256], D, 256)
        wv = stream_w(K, di["ffn_w_up"][l, :, DFF + mp * 256:DFF + (mp + 1) * 256], D, 256)
        pas, pvs = [], []
        for u in range(2):
            pa = next_ps(K)
            for kc in range(16):
                P.mm(pa[:], wa[:, kc, u * 128:(u + 1) * 128], K.XN.sub(kc)[:, sl], start=(kc == 0), stop=(kc == 15))
            pas.append(pa)
        for u in range(2):
            pv_ = next_ps(K)
            for kc in range(16):
                P.mm(pv_[:], wv[:, kc, u * 128:(u + 1) * 128], K.XN.sub(kc)[:, sl], start=(kc == 0), stop=(kc == 15))
            pvs.append(pv_)
        for u in range(2):
            m = mp * 2 + u
            ap_, ac = apad[u], acc[u]
            P.copy("act", ap_[:, :, 1:L + 1], r3(pas[u][:]))
            P.memset("dve", ap_[:, :, 0:1], 0.0)
            P.memset("dve", ap_[:, :, L + 1:L + 2], 0.0)
            cw = lambda kk: pvv(K, "ffn_cw", kk * NFF + m)
            P.ts("dve", ac[:], ap_[:, :, 0:L], cw(0), pvv(K, "ffn_cb", m), ALU.mult, ALU.add)
            P.stt(ac[:], ap_[:, :, 1:L + 1], cw(1), ac[:], ALU.mult, ALU.add)
            P.stt(ac[:], ap_[:, :, 2:L + 2], cw(2), ac[:], ALU.mult, ALU.add)
            P.act(ac[:], ac[:], AF.Gelu_apprx_tanh)
            P.tt("dve", r3(K.HH.sub(m)[:, :]), ac[:], r3(pvs[u][:]), ALU.mult)
    HB = NFF // 2
    for jp in range(8):
        w0 = stream_w(K, di["ffn_w_down"][l, 0:HB * 128, jp * 256:(jp + 1) * 256], HB * 128, 256)
        w1 = stream_w(K, di["ffn_w_down"][l, HB * 128:NFF * 128, jp * 256:(jp + 1) * 256], HB * 128, 256)
        for u in range(2):
            j = jp * 2 + u
            pd = next_ps(K)
            for kc in range(NFF):
                w = w0 if kc < HB else w1
                P.mm(pd[:], w[:, kc % HB, u * 128:(u + 1) * 128], K.HH.sub(kc)[:, :], start=(kc == 0), stop=(kc == NFF - 1))
            P.stt(K.XT.sub(j)[:, sl], pd[:], K.mod[:, 5, j:j + 1], K.XT.sub(j)[:, sl], ALU.mult, ALU.add)


def ffn_sample(K, l, scr_off):
    P, di = K.P, K.di
    HB = NFF // 2
    HS = K.carve("HS", K.A0, 11264, BF16, (HB, 1024), nsub=HB)
    apad = [K.carve("sapad%d" % i, scr_off + i * 2050, 1026) for i in range(2)]
    acc = [K.carve("sfacc%d" % i, scr_off + i * 2050 + 1026, 1024) for i in range(2)]
    for half in range(2):
        for mm_ in range(HB):
            m = half * HB + mm_
            slot = K.WS[K.wi % len(K.WS)]
            K.wi += 1
            w = wview(slot, 16, 256)
            P.dma("pool", w[:, :, 0:128], di["ffn_w_up"][l, :, m * 128:(m + 1) * 128].rearrange("(k p) n -> p k n", p=128), in_dram=True)
            P.dma("pool", w[:, :, 128:256], di["ffn_w_up"][l, :, DFF + m * 128:DFF + (m + 1) * 128].rearrange("(k p) n -> p k n", p=128), in_dram=True)
            ap_, ac = apad[mm_ % 2], acc[mm_ % 2]
            pvs = []
            for tt in range(2):
                sl = slice(tt * 512, (tt + 1) * 512)
                pa = next_ps(K)
                pv_ = next_ps(K)
                for kc in range(16):
                    P.mm(pa[:], w[:, kc, 0:128], K.XN.sub(kc)[:, sl], start=(kc == 0), stop=(kc == 15))
                for kc in range(16):
                    P.mm(pv_[:], w[:, kc, 128:256], K.XN.sub(kc)[:, sl], start=(kc == 0), stop=(kc == 15))
                P.copy("act", ap_[:, 1 + tt * 512:1 + (tt + 1) * 512], pa[:])
                pvs.append(pv_)
            P.memset("dve", ap_[:, 0:1], 0.0)
            P.memset("dve", ap_[:, 1025:1026], 0.0)
            cw = lambda kk: pvv(K, "ffn_cw", kk * NFF + m)
            P.ts("dve", ac[:], ap_[:, 0:1024], cw(0), pvv(K, "ffn_cb", m), ALU.mult, ALU.add)
            P.stt(ac[:], ap_[:, 1:1025], cw(1), ac[:], ALU.mult, ALU.add)
            P.stt(ac[:], ap_[:, 2:1026], cw(2), ac[:], ALU.mult, ALU.add)
            P.act(ac[:], ac[:], AF.Gelu_apprx_tanh)
            for tt in range(2):
                sl = slice(tt * 512, (tt + 1) * 512)
                P.tt("dve", HS.sub(mm_)[:, sl], ac[:, sl], pvs[tt][:], ALU.mult)
        for jp in range(8):
            w = stream_w(K, di["ffn_w_down"][l, half * HB * 128:(half + 1) * HB * 128, jp * 256:(jp + 1) * 256], HB * 128, 256)
            for u in range(2):
                j = jp * 2 + u
                for tt in range(2):
                    sl = slice(tt * 512, (tt + 1) * 512)
                    pd = next_ps(K)
                    for kc in range(HB):
                        P.mm(pd[:], w[:, kc, u * 128:(u + 1) * 128], HS.sub(kc)[:, sl], start=(kc == 0), stop=(kc == HB - 1))
                    P.stt(K.XT.sub(j)[:, sl], pd[:], K.mod[:, 5, j:j + 1], K.XT.sub(j)[:, sl], ALU.mult, ALU.add)


def run_group(K, g):
    P = K.P
    T_ = 512 if g == "p" else 1024
    sub = K.substage
    import os
    if os.environ.get("NO_LOAD") != "1":
        load_x(K, g, T_)
        P.barrier()
    for l in range(2):
        if sub < 2:
            break
        layer_setup(K, l, g)
        rms_modulate(K, T_, 0, 1, K.B0)
        P.barrier()
        if sub < 3:
            continue
        first = True
        for n in BRANCH_ORDER:
            fn = BRANCHES[n]
            if fn is None or n not in K.branches:
                continue
            K.WS, K.ps_hi = K.WS2, 4
            fn(K, l, g, T_, first)
            P.barrier()
            K.WS, K.ps_hi = K.WS4, 6
            merge_branch(K, l, n, T_, first, K.B0)
            P.barrier()
            first = False
        K.WS, K.ps_hi = K.WS4, 6
        if first:
            for j in range(16):
                P.memset("dve", K.MG.sub(j)[:, 0:T_], 0.0)
        out_proj(K, l, T_)
        P.barrier()
        K.WS, K.ps_hi = K.WS3, 6
        if sub < 4:
            continue
        rms_modulate(K, T_, 3, 4, K.B0)
        P.barrier()
        if sub < 5:
            continue
        if g == "p":
            ffn(K, l, g, T_, K.B0)
        else:
            ffn_sample(K, l, K.B0)
        P.barrier()
        K.WS, K.ps_hi = K.WS2, 4
    import os
    if os.environ.get("NO_STORE") != "1":
        store_x(K, g, T_)
        P.barrier()


BRANCHES = [None, None, None, None]
BRANCH_ORDER = [0, 1, 2, 3]


def seq_list(g):
    return [(0, 256), (256, 256)] if g == "p" else [(0, 1024)]


def br_lru(K, l, g, T_, first):
    P, di = K.P, K.di
    seqs = seq_list(g)
    nseq, L = len(seqs), seqs[0][1]
    NT = T_ // 512
    o = K.B0
    lxp = K.carve("lxp", o, nseq * (L + 3), F32, (nseq, L + 3)); o += 1032
    gl = K.carve("lgl", o, T_ // 2, BF16); o += 512
    xc = K.carve("lxc", o, T_, F32); o += 1024
    xcb = K.carve("lxcb", o, T_ // 2, BF16); o += 512
    a_ = K.carve("la", o, T_); o += 1024
    t1 = K.carve("lt1", o, T_); o += 1024
    i_ = K.carve("li", o, T_); o += 1024
    hf = K.carve("lhf", o, T_); o += 1024
    gw = K.carve("lgw", o, 256, BF16, (4, 128)); o += 256
    assert o <= K.B0 + K.BW
    P.act(K.lru_c1[:, 0:16], pvv(K, "lru_lam"), AF.Exp, scale=-1.0)
    P.act(K.lru_c1[:, 0:16], K.lru_c1[:, 0:16], AF.Ln, bias=1.0)
    P.ts("dve", K.lru_c1[:, 0:16], K.lru_c1[:, 0:16], -8.0, None, ALU.mult)
    P.memset("dve", lxp[:, :, 0:2], 0.0)
    P.memset("dve", lxp[:, :, L + 2:L + 3], 0.0)
    xc3 = xc[:].rearrange("p (s l) -> p s l", s=nseq)
    for c in range(8):
        u_ = c % 2
        if u_ == 0:
            wX = stream_w(K, di["w_in"][l, :, OFF["lru_x"] + c * 128:OFF["lru_x"] + (c + 2) * 128], D, 256)
            wG = stream_w(K, di["w_in"][l, :, OFF["lru_g"] + c * 128:OFF["lru_g"] + (c + 2) * 128], D, 256)
        P.dma("pool", gw[:, 0:2, :], di["lru_wa"][l, :, c].rearrange("r d e -> d r e"), in_dram=True)
        P.dma("pool", gw[:, 2:4, :], di["lru_wx"][l, :, c].rearrange("r d e -> d r e"), in_dram=True)
        for tt in range(NT):
            sl = slice(tt * 512, (tt + 1) * 512)
            px = next_ps(K)
            pg = next_ps(K)
            for kc in range(16):
                P.mm(px[:], wX[:, kc, u_ * 128:(u_ + 1) * 128], K.XN.sub(kc)[:, sl], start=(kc == 0), stop=(kc == 15))
            for kc in range(16):
                P.mm(pg[:], wG[:, kc, u_ * 128:(u_ + 1) * 128], K.XN.sub(kc)[:, sl], start=(kc == 0), stop=(kc == 15))
            if g == "p":
                P.copy("act", lxp[:, :, 2:2 + L], px[:].rearrange("p (s l) -> p s l", s=nseq))
            else:
                P.copy("act", lxp[:, 0, 2 + tt * 512:2 + (tt + 1) * 512], px[:])
            P.act(gl[:, sl], pg[:], AF.Gelu_apprx_tanh)
        cw = lambda kk: pvv(K, "lru_cw", kk * 8 + c)
        P.ts("dve", xc3, lxp[:, :, 0:L], cw(0), pvv(K, "lru_cb", c), ALU.mult, ALU.add)
        for kk in range(1, 4):
            P.stt(xc3, lxp[:, :, kk:kk + L], cw(kk), xc3, ALU.mult, ALU.add)
        P.copy("act", xcb[:], xc[:])
        for d in range(2):
            for tt in range(NT):
                sl = slice(tt * 512, (tt + 1) * 512)
                pr = next_ps(K)
                pi = next_ps(K)
                P.mm(pr[:], gw[:, d, :], xcb[:, sl], start=True, stop=True)
                P.mm(pi[:], gw[:, 2 + d, :], xcb[:, sl], start=True, stop=True)
                P.act(a_[:, sl], pr[:], AF.Sigmoid, bias=pvv(K, "lru_ba", d * 8 + c))
                P.act(i_[:, sl], pi[:], AF.Sigmoid, bias=pvv(K, "lru_bx", d * 8 + c))
            P.act(a_[:], a_[:], AF.Exp, scale=K.lru_c1[:, d * 8 + c:d * 8 + c + 1])
            P.tt("dve", t1[:], a_[:], a_[:], ALU.mult)
            P.act(t1[:], t1[:], AF.Ln, scale=-1.0, bias=1.0)
            P.act(t1[:], t1[:], AF.Exp, scale=0.5)
            P.tt("dve", i_[:], i_[:], xc[:], ALU.mult)
            P.tt("dve", i_[:], i_[:], t1[:], ALU.mult)
            hdst = hf if d == 0 else t1
            for si, (s0, Ls) in enumerate(seqs):
                if g == "p":
                    init = 0.0
                else:
                    col = (d * 2 + l) * 8 + c
                    init = K.lru_h0[:, col:col + 1]
                if d == 0:
                    P.scan(hdst[:, s0:s0 + Ls], a_[:, s0:s0 + Ls], i_[:, s0:s0 + Ls], init)
                else:
                    P.scan(hdst[:, s0:s0 + Ls][:, ::-1], a_[:, s0:s0 + Ls][:, ::-1], i_[:, s0:s0 + Ls][:, ::-1], init)
                if g == "p":
                    col = d * 32 + si * 16 + l * 8 + c
                    src_col = s0 + Ls - 1 if d == 0 else s0
                    P.copy("dve", K.stT[:, col:col + 1], hdst[:, src_col:src_col + 1])
        P.tt("dve", hf[:], hf[:], t1[:], ALU.add)
        P.tt("dve", K.YB.sub(c)[:, 0:T_], hf[:], gl[:], ALU.mult)


BRANCHES[0] = br_lru


def br_na(K, l, g, T_, first):
    P, di = K.P, K.di
    seqs = seq_list(g)
    NG = T_ // 512
    NCK = T_ // 128
    o = K.B0
    qT = K.carve("nqT", o, T_ // 2, BF16); o += 512
    kT = K.carve("nkT", o, T_ // 2, BF16); o += 512
    vh = K.carve("nvh", o, NCK * 64, BF16, (NCK, 128)); o += 512
    sq = K.carve("nsq", o, 1024, F32, (8, 128)); o += 1024
    ss = K.carve("nss", o, 8); o += 8
    qn = K.carve("nqn", o, 1024, F32, (8, 128)); o += 1024
    eeqb = K.carve("neeqb", o, 512, BF16); o += 512
    qb = V(eeqb.h.rearrange("p (a b) -> p a b", b=128), eeqb.bufs)
    gains = K.carve("ngain", o, 256, F32, (2, 128)); o += 256
    vf = None
    if g == "p":
        vf = K.carve("nvf", o, 512, F32, (4, 128)); o += 512
    ee = [eeqb[:, 0:512], eeqb[:, 512:1024]]
    rden = None
    if g == "p":
        rden = K.carve("nrden", o, 512); o += 512
    if g == "s":
        rope = K.carve("nrope", o, 512, F32, (4, 128)); o += 512
        tAB = K.carve("ntAB", o, 512); o += 512
        tA = tAB[:, 0:256]
        tB = tAB[:, 256:512]
        NS_ = Ctx()
        NS_.T3f = K.carve("nT3f", o, 512, BF16, (16, 64)); o += 512
        NS_.T3m = K.carve("nT3m", o, 512, BF16, (16, 64)); o += 512
        NS_.kctok = K.carve("nkctok", o, 128, BF16, (2, 128)); o += 128
        NS_.kcT = K.carve("nkcT", o, 128, BF16); o += 128
        NS_.vc = K.carve("nvc", o, 128, BF16, (2, 128)); o += 128
        NS_.cmask = K.carve("ncmask", o, 64); o += 64
        NS_.tAB, NS_.qn, NS_.sq = tAB, qn, sq
        P.dma("sp", NS_.cmask[:], di["na_colmask"], in_dram=True)
    assert o <= K.B0 + K.BW, (o - K.B0, K.BW)
    P.dma("sp", gains[:], di["nag"][l], in_dram=True)
    scale = 128 ** -0.5
    for h in range(8):
        slot = K.WS[K.wi % 2]
        K.wi += 1
        w = wview(slot, 16, 384)
        for i, nm in enumerate(("na_q", "na_k", "na_v")):
            c0 = OFF[nm] + h * 128
            P.dma("pool", w[:, :, i * 128:(i + 1) * 128], di["w_in"][l, :, c0:c0 + 128].rearrange("(k p) n -> p k n", p=128), in_dram=True)
        for tg in range(NG):
            pq, pk, pvv_ = K.PS[4], K.PS[5], K.PS[6]
            for tci in range(4):
                tc_ = tg * 4 + tci
                tsl = slice(tc_ * 128, (tc_ + 1) * 128)
                for i, pb in enumerate((pq, pk, pvv_)):
                    for kc in range(16):
                        P.mm(pb[:, tci * 128:(tci + 1) * 128], K.XN.sub(kc)[:, tsl], w[:, kc, i * 128:(i + 1) * 128],
                             start=(kc == 0), stop=(kc == 15))
            r3 = lambda v: v.rearrange("p (a b) -> p a b", b=128)
            P.act(sq[:, 0:4, :], r3(pq[:]), AF.Square)
            P.act(sq[:, 4:8, :], r3(pk[:]), AF.Square)
            P.emit("dve", lambda e, o_=ss[:].ap, i_=sq[:].ap: e.tensor_reduce(o_, i_, AX.X, ALU.add), [sq[:]], [ss[:]])
            P.act(ss[:], ss[:], AF.Ln, scale=1.0 / 128, bias=EPS)
            P.act(ss[:], ss[:], AF.Exp, scale=-0.5)
            for i, pb in enumerate((pq, pk)):
                P.tt("dve", qn[:, i * 4:(i + 1) * 4, :], r3(pb[:]), ss[:, i * 4:(i + 1) * 4].unsqueeze(2).to_broadcast([128, 4, 128]), ALU.mult)
                P.tt("dve", qn[:, i * 4:(i + 1) * 4, :], qn[:, i * 4:(i + 1) * 4, :],
                     gains[:, i, :].unsqueeze(1).to_broadcast([128, 4, 128]), ALU.mult)
            P.copy("act", vh[:, tg * 4:(tg + 1) * 4, :], r3(pvv_[:]))
            if g == "p":
                P.copy("dve", vf[:], r3(pvv_[:]))
                b = tg
                for si in range(2):
                    P.dma("sp", di["nk"][si, l, :, h * 128:(h + 1) * 128].rearrange("(c p) d -> p c d", p=128),
                          qn[:, 4 + si * 2:4 + si * 2 + 2, :], out_dram=True)
                    P.dma("sp", di["nv"][si, l, :, h * 128:(h + 1) * 128].rearrange("(c p) d -> p c d", p=128),
                          vf[:, si * 2:si * 2 + 2, :], out_dram=True)
                P.copy("act", qb, qn[:])
            else:
                P.dma("sp", rope[:], di["rope_cs"][:, tg * 4:(tg + 1) * 4].rearrange("p c a h f -> p c (a h f)"), in_dram=True)
                for i in range(2):
                    xv = qn[:, i * 4:(i + 1) * 4, :].rearrange("p c (h x f) -> p c h x f", h=2, x=2)
                    ov = qb[:, i * 4:(i + 1) * 4, :].rearrange("p c (h x f) -> p c h x f", h=2, x=2)
                    tb = rope[:].rearrange("p c (a h f) -> p c a h f", a=2, h=2)
                    cos, sin = tb[:, :, 0], tb[:, :, 1]
                    x1, x2 = xv[:, :, :, 0, :], xv[:, :, :, 1, :]
                    a4 = tA.rearrange("p (c h f) -> p c h f", c=4, h=2)
                    b4 = tB.rearrange("p (c h f) -> p c h f", c=4, h=2)
                    P.tt("dve", a4, x1, cos, ALU.mult)
                    P.tt("dve", b4, x2, sin, ALU.mult)
                    P.tt("dve", ov[:, :, :, 0, :], a4, b4, ALU.subtract)
                    P.tt("dve", a4, x1, sin, ALU.mult)
                    P.tt("dve", b4, x2, cos, ALU.mult)
                    P.tt("dve", ov[:, :, :, 1, :], a4, b4, ALU.add)
            pt = K.PS[7]
            ptb = V(pt.h[:].bitcast(BF16), pt.bufs)
            for i in range(2):
                for tci in range(4):
                    P.transpose(ptb[:, (i * 4 + tci) * 128:(i * 4 + tci + 1) * 128], qb[:, i * 4 + tci, :], K.ident_b[:])
            P.copy("act", qT[:, tg * 512:(tg + 1) * 512], ptb[:, 0:512])
            P.copy("dve", kT[:, tg * 512:(tg + 1) * 512], ptb[:, 512:1024])
        if g == "p":
            for si, (s0, L) in enumerate(seqs):
                po, pd = K.PS[4], K.PS[5]
                for i in range(2):
                    ck = si * 2 + i
                    pS = next_ps(K)
                    P.mm(pS[:, 0:256], kT[:, ck * 128:(ck + 1) * 128], qT[:, s0:s0 + 256], start=True, stop=True)
                    e_ = ee[i]
                    P.act(e_[:, 0:256], pS[:, 0:256], AF.Exp, scale=scale)
                    P.mm(po[:, 0:256], vh[:, ck, :], e_[:, 0:256], start=(i == 0), stop=(i == 1))
                    P.mm(pd[:, 0:256], K.ones_b[:], e_[:, 0:256], start=(i == 0), stop=(i == 1))
                P.recip(rden[:, 0:256], pd[:, 0:256])
                P.tt("dve", K.YB.sub(h)[:, s0:s0 + 256], po[:, 0:256], rden[:, 0:256], ALU.mult)
        else:
            na_sample_attn(K, l, h, qT, kT, vh, ee, rden, scale, NS_)


def na_sample_tables(K, l, h, N):
    P, di = K.P, K.di
    R = V(N.qn.h[:, 0:8, :].rearrange("p a b -> p (a b)")[:, 0:960].rearrange("p (a j) -> p a j", j=64), N.qn.bufs)
    T2 = V(N.tAB.h[:].bitcast(BF16)[:, 0:960].rearrange("p (a j) -> p a j", j=64), N.tAB.bufs)
    rp = di["rpbp"]
    base = (l * 8 + h) * 15 * 128
    src = bass.AP(rp.tensor, base, [[1, 64], [128, 15], [1, 64]])
    P.dma("sp", R[0:64], src, in_dram=True)
    P.dma("sp", R[64:128], src, in_dram=True)
    P.act(R, R, AF.Exp)
    P.tt("dve", T2, R[:, :, ::-1], N.cmask[:].unsqueeze(1).to_broadcast([128, 15, 64]), ALU.mult)
    P.memset("dve", N.T3f[:], 0.0)
    P.memset("dve", N.T3m[:], 0.0)
    P.copy("dve", N.T3f[0:64, 0:15, :], T2[0:64, ::-1, :])
    P.copy("dve", N.T3f[64:128, 1:16, :], T2[64:128, ::-1, :])
    P.copy("dve", N.T3m[0:64, 4:12, :], T2[0:64, 10:2:-1, :])
    P.copy("dve", N.T3m[64:128, 5:13, :], T2[64:128, 10:2:-1, :])
    P.dma("pool", N.kctok[:], di["cache_k"][l, :, h * 128:(h + 1) * 128].rearrange("(c p) d -> p c d", p=128), in_dram=True)
    P.dma("pool", N.vc[:], di["cache_v"][l, :, h * 128:(h + 1) * 128].rearrange("(c p) d -> p c d", p=128), in_dram=True)
    pt = K.PS[7]
    ptb = V(pt.h[:].bitcast(BF16), pt.bufs)
    for c in range(2):
        P.transpose(ptb[:, c * 128:(c + 1) * 128], N.kctok[:, c, :], K.ident_b[:])
    P.copy("act", N.kcT[:], ptb[:, 0:256])


def na_sample_attn(K, l, h, qT, kT, vh, ee, rden, scale, N):
    P = K.P
    na_sample_tables(K, l, h, N)
    Eb = [V(N.sq.h[:, 0:4, :].rearrange("p a b -> p (a b)"), N.sq.bufs), V(N.sq.h[:, 4:8, :].rearrange("p a b -> p (a b)"), N.sq.bufs)]
    for j in range(2):
        qsl = slice(j * 512, (j + 1) * 512)
        chunks = list(range(0, 6)) if j == 0 else list(range(2, 8))
        po, pd = K.PS[4], K.PS[5]
        nacc = len(chunks) + 2
        step = 0
        for i in chunks:
            pS = next_ps(K)
            P.mm(pS[:], kT[:, i * 128:(i + 1) * 128], qT[:, qsl], start=True, stop=True)
            E = Eb[step % 2]
            P.act(E, pS[:], AF.Exp, scale=scale)
            pb = ee[step % 2]
            segs = []
            for qr in range(8 * j, 8 * j + 8):
                e1 = qr - 2 * i + 7
                if qr <= 3:
                    kind = "f" if i <= 3 else "z"
                elif qr >= 13:
                    kind = "f" if i >= 4 else "z"
                else:
                    kind = "m" if 0 <= e1 <= 15 else "z"
                if segs and segs[-1][0] == kind:
                    segs[-1][2] = qr
                else:
                    segs.append([kind, qr, qr])
            for kind, qa, qb_ in segs:
                c0, c1 = (qa - 8 * j) * 64, (qb_ - 8 * j + 1) * 64
                if kind == "z":
                    P.memset("dve", pb[:, c0:c1], 0.0)
                else:
                    tab = N.T3f if kind == "f" else N.T3m
                    ea, eb = qa - 2 * i + 7, qb_ - 2 * i + 7
                    P.tt("dve", pb[:, c0:c1].rearrange("p (a b) -> p a b", b=64), E[:, c0:c1].rearrange("p (a b) -> p a b", b=64),
                         tab[:, ea:eb + 1, :], ALU.mult)
            P.mm(po[:], vh[:, i, :], pb, start=(step == 0), stop=False)
            P.mm(pd[:], K.ones_b[:], pb, start=(step == 0), stop=False)
            step += 1
        for c in range(2):
            pS = next_ps(K)
            P.mm(pS[:], N.kcT[:, c * 128:(c + 1) * 128], qT[:, qsl], start=True, stop=True)
            pb = ee[step % 2]
            P.act(pb, pS[:], AF.Exp, scale=scale)
            P.mm(po[:], N.vc[:, c, :], pb, start=False, stop=(c == 1))
            P.mm(pd[:], K.ones_b[:], pb, start=False, stop=(c == 1))
            step += 1
        P.recip(Eb[0], pd[:])
        P.tt("dve", K.YB.sub(h)[:, qsl], po[:], Eb[0], ALU.mult)


BRANCHES[3] = br_na


def br_ret(K, l, g, T_, first):
    P, di = K.P, K.di
    seqs = seq_list(g)
    NT = T_ // 512
    NCK = T_ // 128
    o = K.B0
    qT = K.carve("rqT", o, T_ // 2, BF16); o += 512
    kT = K.carve("rkT", o, T_ // 2, BF16); o += 512
    vh = K.carve("rvh", o, NCK * 128, BF16, (NCK, 256)); o += 1024
    gs = K.carve("rgs", o, 512, BF16, (2, 512)); o += 512
    mk = K.carve("rmk", o, 960, BF16); o += 960
    X = [K.carve("rX%d" % i, o + i * 256, 256, BF16) for i in range(4)]; o += 1024
    rows = K.carve("rrows", o, 512, BF16, (2, 512)); o += 512
    s0b = K.carve("rs0", o, 256, BF16, (2, 256)); o += 256
    yf = K.carve("ryf", o, 1024, F32, (2, 512)); o += 1024
    tm = [K.carve("rtm%d" % i, o + i * 512, 512) for i in range(3)]; o += 1536
    tail = K.carve("rtail", o, 16); o += 16
    assert o <= K.B0 + K.BW, (o - K.B0, K.BW)
    P.dma("sp", tail[:], di["ret_tail"], in_dram=True)
    for h in range(4):
        slot = K.WS[K.wi % 2]
        K.wi += 1
        wqk = wview(slot, 16, 256)
        for i, nm in enumerate(("ret_q", "ret_k")):
            c0 = OFF[nm] + h * 128
            P.dma("pool", wqk[:, :, i * 128:(i + 1) * 128], di["w_in"][l, :, c0:c0 + 128].rearrange("(k p) n -> p k n", p=128), in_dram=True)
        for tt in range(NT):
            sl = slice(tt * 512, (tt + 1) * 512)
            pq, pk = next_ps(K), next_ps(K)
            for kc in range(16):
                P.mm(pq[:], wqk[:, kc, 0:128], K.XN.sub(kc)[:, sl], start=(kc == 0), stop=(kc == 15))
            for kc in range(16):
                P.mm(pk[:], wqk[:, kc, 128:256], K.XN.sub(kc)[:, sl], start=(kc == 0), stop=(kc == 15))
            P.copy("act", qT[:, sl], pq[:])
            P.copy("dve", kT[:, sl], pk[:])
        wv = stream_w(K, di["w_in"][l, :, OFF["ret_v"] + h * 256:OFF["ret_v"] + (h + 1) * 256], D, 256)
        for tp in range(NCK // 2):
            pv_ = next_ps(K)
            for u in range(2):
                tc_ = tp * 2 + u
                for kc in range(16):
                    P.mm(pv_[:, u * 256:(u + 1) * 256], K.XN.sub(kc)[:, tc_ * 128:(tc_ + 1) * 128], wv[:, kc, :],
                         start=(kc == 0), stop=(kc == 15))
            P.copy("act", vh[:, tp * 2:tp * 2 + 2, :], pv_[:].rearrange("p (a b) -> p a b", b=256))
        wg = stream_w(K, di["w_in"][l, :, OFF["ret_g"] + h * 256:OFF["ret_g"] + (h + 1) * 256], D, 256)
        P.dma("pool", mk[:], di["ret_mask"][h], in_dram=True)
        if g == "s":
            P.dma("pool", s0b[:], di["ret_s0"][:, l, h].rearrange("r k v -> k r v"), in_dram=True)
        for si, (s0, L) in enumerate(seqs):
            TW = min(L, 512)
            NS = L // 128
            for j in range(L // TW):
                t0 = s0 + j * TW
                tsl = slice(t0, t0 + TW)
                for dvc in range(2):
                    pg = next_ps(K)
                    for kc in range(16):
                        P.mm(pg[:, 0:TW], wg[:, kc, dvc * 128:(dvc + 1) * 128], K.XN.sub(kc)[:, tsl], start=(kc == 0), stop=(kc == 15))
                    P.act(gs[:, dvc, 0:TW], pg[:, 0:TW], AF.Silu)
                if g == "s":
                    P.dma("pool", rows[:], di["ret_rows"][:, h:8:4, j * 512:(j + 1) * 512], in_dram=True)
                pY = [K.PS[4], K.PS[5]]
                for i in range(NS):
                    pS = next_ps(K)
                    P.mm(pS[:, 0:TW], kT[:, s0 + i * 128:s0 + (i + 1) * 128], qT[:, tsl], start=True, stop=True)
                    base = j * TW - 128 * i + 896
                    pt = X[i % 2]
                    P.tt("dve", pt[:, 0:TW], pS[:, 0:TW], mk[:, base:base + TW], ALU.mult)
                    for dvc in range(2):
                        P.mm(pY[dvc][:, 0:TW], vh[:, s0 // 128 + i, dvc * 128:(dvc + 1) * 128], pt[:, 0:TW],
                             start=(i == 0), stop=(i == NS - 1 and g == "p"))
                if g == "s":
                    for d in range(2):
                        qf = X[2 + d]
                        P.tt("dve", qf[:, 0:TW], qT[:, tsl], rows[:, d, 0:TW], ALU.mult)
                        for dvc in range(2):
                            P.mm(pY[dvc][:, 0:TW], s0b[:, d, dvc * 128:(dvc + 1) * 128], qf[:, 0:TW], start=False, stop=(d == 1))
                for dvc in range(2):
                    P.copy("act", yf[:, dvc, 0:TW], pY[dvc][:, 0:TW])
                    P.copy("dve", X[dvc][:, 0:TW], pY[dvc][:, 0:TW])
                    P.act(X[2 + dvc][:, 0:TW], pY[dvc][:, 0:TW], AF.Square)
                pm, pq2 = K.PS[6], K.PS[7]
                for dvc in range(2):
                    P.mm(pm[:, 0:TW], K.ones_b[:], X[dvc][:, 0:TW], start=(dvc == 0), stop=(dvc == 1))
                for dvc in range(2):
                    P.mm(pq2[:, 0:TW], K.ones_b[:], X[2 + dvc][:, 0:TW], start=(dvc == 0), stop=(dvc == 1))
                m_, v_, r_ = tm[0], tm[1], tm[2]
                P.act(m_[:, 0:TW], pm[:, 0:TW], AF.Identity, scale=1.0 / 256)
                P.tt("dve", v_[:, 0:TW], m_[:, 0:TW], m_[:, 0:TW], ALU.mult)
                P.stt(v_[:, 0:TW], pq2[:, 0:TW], 1.0 / 256, v_[:, 0:TW], ALU.mult, ALU.subtract)
                P.act(r_[:, 0:TW], v_[:, 0:TW], AF.Ln, bias=EPS)
                P.act(r_[:, 0:TW], r_[:, 0:TW], AF.Exp, scale=-0.5)
                for dvc in range(2):
                    y_ = yf[:, dvc, 0:TW]
                    P.tt("dve", y_, y_, m_[:, 0:TW], ALU.subtract)
                    P.tt("dve", y_, y_, r_[:, 0:TW], ALU.mult)
                    P.stt(K.YB.sub(h * 2 + dvc)[:, tsl], y_, pvv(K, "ret_gn", h * 2 + dvc), gs[:, dvc, 0:TW], ALU.mult, ALU.mult)
            if g == "p":
                pst = [K.PS[4], K.PS[5]]
                for i in range(2):
                    ptb_t = K.PS[6]
                    ptb = V(ptb_t.h[:].bitcast(BF16), ptb_t.bufs)
                    P.transpose(ptb[:, 0:128], kT[:, s0 + i * 128:s0 + (i + 1) * 128], K.ident_b[:])
                    for d in range(2):
                        ks = X[d]
                        col = (i * 2 + d) * 4 + h
                        P.ts("dve", ks[:, 0:128], ptb[:, 0:128], tail[:, col:col + 1], None, ALU.mult)
                        P.mm(pst[d][:, 0:256], ks[:, 0:128], vh[:, si * 2 + i, :], start=(i == 0), stop=(i == 1))
                for d in range(2):
                    P.copy("act" if d else "dve", yf[:, d, 0:256], pst[d][:, 0:256])
                    P.dma("sp", di["nret"][d, si, l, h], yf[:, d, 0:256], out_dram=True)


BRANCHES[1] = br_ret


def br_ssd(K, l, g, T_, first):
    P, di = K.P, K.di
    seqs = seq_list(g)
    nseq, L = len(seqs), seqs[0][1]
    NT = T_ // 512
    NCK = T_ // 128
    TW = min(L, 512)
    NS = L // 128
    NJ = L // TW
    H2 = T_ // 2
    o = K.A0
    Gsb = [K.carve("sG%d" % i, o + i * H2, H2, BF16) for i in range(NS)]; o += NS * H2
    parts = [K.carve("spart%d" % i, o + i * H2, H2, BF16) for i in range(3)]; o += 3 * H2
    biasT = K.carve("sbiasT", o, T_); o += T_
    onesL = K.carve("sones", o, H2, BF16, None, nsub=2); o += H2
    BCT = K.carve("sBCT", o, 2 * H2, BF16, (2, T_)); o += 2 * H2
    assert o <= K.A0 + 8192, o - K.A0
    o = K.B0
    cum = K.carve("scum", o, T_); o += T_
    ytmp = K.carve("sytmp", o, T_); o += T_
    btok = K.carve("sbtok", o, NCK * 64, F32, (NCK, 64)); o += NCK * 64
    if g == "p":
        wtok = K.carve("swtok", o, NCK * 64, F32, (NCK, 64)); o += NCK * 64
        Btok = K.carve("sBtok", o, NCK * 64, BF16, (NCK, 128)); o += NCK * 64
    xsT = K.carve("sxsT", o, H2, BF16); o += H2
    xtok = K.carve("sxtok", o, NCK * 64, BF16, (NCK, 128)); o += NCK * 64
    zs = K.carve("szs", o, H2, BF16); o += H2
    scrA = K.carve("sscrA", o, 2048, F32, None, nsub=6); o += 2048
    sel2 = [K.carve("ssel%d" % i, o + i * 64, 64, BF16) for i in range(2)]; o += 128
    Cf = K.carve("sCf", o, 256, BF16); o += 256
    ec = K.carve("sec", o, 256, BF16); o += 256
    s0h = K.carve("ss0h", o, 128, BF16, (4, 64)); o += 128
    caus = K.carve("scaus", o, 896, BF16, (2, 896)); o += 896
    small = K.carve("ssmall", o, 8); o += 8
    assert o <= K.B0 + K.BW, (o - K.B0, K.BW)
    Mf = [V(scrA.h[:, 0:256].bitcast(BF16), [scrA.bufs[0]]), V(scrA.h[:, 256:512].bitcast(BF16), [scrA.bufs[1]])]
    Mb = [V(scrA.h[:, 512:768].bitcast(BF16), [scrA.bufs[2]]), V(scrA.h[:, 768:1024].bitcast(BF16), [scrA.bufs[3]])]
    tmpc = [V(scrA.h[:, 1024:1536], [scrA.bufs[4]]), V(scrA.h[:, 1536:2048], [scrA.bufs[5]])]
    Pb = [V(onesL.h[:, 0:TW], [onesL.bufs[0]]), V(onesL.h[:, TW:2 * TW], [onesL.bufs[1]])]
    cpad = V(scrA.h[:, 0:nseq * (L + 3)].rearrange("p (s l) -> p s l", s=nseq), scrA.bufs)
    dtT = ytmp
    P.dma("pool", caus[:], di["ssd_causal"], in_dram=True)
    P.memset("dve", onesL[:], 1.0)
    P.memset("dve", cpad[:, :, 0:2], 0.0)
    P.memset("dve", cpad[:, :, L + 2:L + 3], 0.0)

    def conv_chunk(cidx, dst_bf):
        c0 = OFF["ssd_xbc"] + cidx * 128
        w = stream_w(K, di["w_in"][l, :, c0:c0 + 128], D, 128)
        P.memset("dve", cpad[:, :, 0:2], 0.0)
        P.memset("dve", cpad[:, :, L + 2:L + 3], 0.0)
        for tt in range(NT):
            sl = slice(tt * 512, (tt + 1) * 512)
            px = next_ps(K)
            for kc in range(16):
                P.mm(px[:], w[:, kc, :], K.XN.sub(kc)[:, sl], start=(kc == 0), stop=(kc == 15))
            if g == "p":
                P.copy("act", cpad[:, :, 2:2 + L], px[:].rearrange("p (s l) -> p s l", s=nseq))
            else:
                P.copy("act", cpad[:, 0, 2 + tt * 512:2 + (tt + 1) * 512], px[:])
        acc = ytmp[:].rearrange("p (s l) -> p s l", s=nseq)
        cw = lambda kk: pvv(K, "ssd_cw", kk * 12 + cidx)
        P.ts("dve", acc, cpad[:, :, 0:L], cw(0), pvv(K, "ssd_cb", cidx), ALU.mult, ALU.add)
        for kk in range(1, 4):
            P.stt(acc, cpad[:, :, kk:kk + L], cw(kk), acc, ALU.mult, ALU.add)
        P.act(dst_bf, ytmp[:], AF.Silu)

    def split_T(src64, cols):
        split3(K, src64, [p_[0:64, 0:cols] for p_ in parts], biasT[0:64, 0:cols] if False else tmp64[0:64, 0:cols])

    tmp64 = K.carve("stmp64", K.A0 + NS * H2 + 3 * H2, T_)
    tmp64 = biasT

    wdt = stream_w(K, di["w_in"][l, :, OFF["ssd_dt"]:OFF["ssd_dt"] + 32], D, 32)
    P.memset("dve", dtT[0:64, :], 1.718281828)
    for tt in range(NT):
        sl = slice(tt * 512, (tt + 1) * 512)
        pdt = next_ps(K)
        for d in range(2):
            for kc in range(16):
                P.mm(pdt[d * 32:d * 32 + 16, :], wdt[:, kc, d * 16:(d + 1) * 16], K.XN.sub(kc)[:, sl], start=(kc == 0), stop=(kc == 15))
        for d in range(2):
            P.act(dtT[d * 32:d * 32 + 16, sl], pdt[d * 32:d * 32 + 16, :], AF.Exp, bias=pvv(K, "ssd_dtb")[d * 32:d * 32 + 16])
    P.act(dtT[0:64, :], dtT[0:64, :], AF.Ln, bias=1.0)
    P.act(small[0:64, 0:1], pvv(K, "ssd_alog")[0:64], AF.Exp)
    P.ts("dve", small[0:64, 0:1], small[0:64, 0:1], -1.0, None, ALU.mult)
    P.ts("dve", biasT[0:64, :], dtT[0:64, :], small[0:64, 0:1], None, ALU.mult)
    P.memset("dve", cum[0:64, :], 0.0)
    for (s0, Ls) in seqs:
        P.scan(cum[0:16, s0:s0 + Ls], onesL[0:16, s0:s0 + Ls], biasT[0:16, s0:s0 + Ls], 0.0)
        P.scan(cum[32:48, s0:s0 + Ls][:, ::-1], onesL[32:48, s0:s0 + Ls][:, ::-1], biasT[32:48, s0:s0 + Ls][:, ::-1], 0.0)
    P.act(dtT[0:64, :], dtT[0:64, :], AF.Ln)
    P.tt("dve", biasT[0:64, :], dtT[0:64, :], cum[0:64, :], ALU.subtract)

    def to_tok(dst):
        for c4 in range(0, NCK, 8):
            pb_ = next_ps(K)
            n = min(8, NCK - c4)
            for u in range(n):
                ck = c4 + u
                for i3 in range(3):
                    P.mm(pb_[:, u * 64:(u + 1) * 64], parts[i3][0:64, ck * 128:(ck + 1) * 128], K.ident_b[0:64, 0:64],
                         start=(i3 == 0), stop=(i3 == 2))
            P.copy("act", dst[:, c4:c4 + n, :], pb_[:, 0:n * 64].rearrange("p (a b) -> p a b", b=64))

    split3(K, biasT[0:64, :], [p_[0:64, :] for p_ in parts], dtT[0:64, :])
    to_tok(btok)
    if g == "p":
        for (s0, Ls) in seqs:
            P.act(biasT[0:16, s0:s0 + Ls], biasT[0:16, s0:s0 + Ls], AF.Exp, bias=cum[0:16, s0 + Ls - 1:s0 + Ls])
            P.act(biasT[32:48, s0:s0 + Ls], biasT[32:48, s0:s0 + Ls], AF.Exp, bias=cum[32:48, s0:s0 + 1])
        split3(K, biasT[0:64, :], [p_[0:64, :] for p_ in parts], dtT[0:64, :])
        to_tok(wtok)
    split3(K, cum[0:64, :], [p_[0:64, :] for p_ in parts], dtT[0:64, :])

    NSTAT = nseq * NJ
    pstat = [K.PS[6], K.PS[7]]
    assert NSTAT == 2
    for grp in range(2):
        conv_chunk(8 + grp, BCT[:, 0, :])
        conv_chunk(10 + grp, BCT[:, 1, :])
        BT, CT = BCT[:, 0, :], BCT[:, 1, :]
        for si, (s0, Ls) in enumerate(seqs):
            for i in range(NS):
                for j in range(NJ):
                    pG = next_ps(K)
                    P.mm(pG[:, 0:TW], BT[:, s0 + i * 128:s0 + (i + 1) * 128], CT[:, s0 + j * TW:s0 + (j + 1) * TW], start=True, stop=True)
                    P.copy("act" if (i + j) % 2 else "dve", Gsb[i][:, s0 + j * TW:s0 + (j + 1) * TW], pG[:, 0:TW])
        if g == "p":
            ptt = K.PS[5]
            ptb = V(ptt.h[:].bitcast(BF16), ptt.bufs)
            for ck in range(NCK):
                P.transpose(ptb[:, ck * 128:(ck + 1) * 128], BT[:, ck * 128:(ck + 1) * 128], K.ident_b[:])
            P.copy("act", Btok[:], ptb[:, 0:NCK * 128].rearrange("p (a b) -> p a b", b=128))
        for cp in range(4):
            c = grp * 4 + cp
            conv_chunk(c, xsT[:])
            ptt = K.PS[5]
            ptb = V(ptt.h[:].bitcast(BF16), ptt.bufs)
            for ck in range(NCK):
                P.transpose(ptb[:, ck * 128:(ck + 1) * 128], xsT[:, ck * 128:(ck + 1) * 128], K.ident_b[:])
            P.copy("act", xtok[:], ptb[:, 0:NCK * 128].rearrange("p (a b) -> p a b", b=128))
            wz = stream_w(K, di["w_in"][l, :, OFF["ssd_z"] + c * 128:OFF["ssd_z"] + (c + 1) * 128], D, 128)
            for tt in range(NT):
                sl = slice(tt * 512, (tt + 1) * 512)
                pz = next_ps(K)
                for kc in range(16):
                    P.mm(pz[:], wz[:, kc, :], K.XN.sub(kc)[:, sl], start=(kc == 0), stop=(kc == 15))
                P.act(zs[:, sl], pz[:], AF.Silu)
            if g == "s":
                for d in range(2):
                    P.dma("pool", s0h[:, d * 2:d * 2 + 2, :], di["ssd_s0"][d, l, 2 * c:2 * c + 2].rearrange("h n p -> n h p"), in_dram=True)
            for si, (s0, Ls) in enumerate(seqs):
                for j in range(NJ):
                    t0 = s0 + j * TW
                    tsl = slice(t0, t0 + TW)
                    pY = K.PS[4]
                    for hh in range(2):
                        h = 2 * c + hh
                        yv = pY[hh * 64:(hh + 1) * 64, 0:TW]
                        pc = [next_ps(K), next_ps(K)]
                        for d in range(2):
                            row = d * 32 + h
                            sel = sel2[d]
                            P.copy("dve", sel[0:64, :], K.ident_b[0:64, row:row + 1].to_broadcast([64, 128]))
                            for i3 in range(3):
                                P.mm(pc[d][:, 0:TW], sel[0:64, :], parts[i3][0:64, tsl], start=(i3 == 0), stop=(i3 == 2))
                        for i in range(NS):
                            d0 = 128 * i - j * TW
                            need_f = d0 <= TW - 1
                            full_f = d0 + 127 <= 0
                            need_b = d0 + 127 >= 0
                            full_b = d0 >= TW - 1
                            off = 384 - d0
                            ck = s0 // 128 + i
                            ms = []
                            for d, need, full, Mx in ((0, need_f, full_f, Mf[i % 2]), (1, need_b, full_b, Mb[i % 2])):
                                if not need:
                                    continue
                                bcol = btok[:, ck, d * 32 + h:d * 32 + h + 1]
                                if full:
                                    P.act(Mx[:, 0:TW], pc[d][:, 0:TW], AF.Exp, bias=bcol)
                                else:
                                    tq = tmpc[d]
                                    P.tt("dve", tq[:, 0:TW], pc[d][:, 0:TW], caus[:, d, off:off + TW], ALU.add)
                                    P.act(Mx[:, 0:TW], tq[:, 0:TW], AF.Exp, bias=bcol)
                                ms.append(Mx)
                            pb_ = Pb[i % 2]
                            if len(ms) == 2:
                                P.tt("dve", ms[0][:, 0:TW], ms[0][:, 0:TW], ms[1][:, 0:TW], ALU.add)
                            P.tt("dve", pb_[:, 0:TW], Gsb[i][:, tsl], ms[0][:, 0:TW], ALU.mult)
                            P.mm(yv, xtok[:, ck, hh * 64:(hh + 1) * 64], pb_[:, 0:TW], start=(i == 0), stop=(i == NS - 1 and g == "p"))
                        if g == "s":
                            for d in range(2):
                                P.act(ec[:, 0:TW], pc[d][:, 0:TW], AF.Exp)
                                P.tt("dve", Cf[:, 0:TW], CT[:, tsl], ec[:, 0:TW], ALU.mult)
                                P.mm(yv, s0h[:, d * 2 + hh, :], Cf[:, 0:TW], start=False, stop=(d == 1))
                    y1 = ytmp[:, 0:TW]
                    P.stt(y1, xsT[:, tsl], pvv(K, "ssd_dd", c), pY[:, 0:TW], ALU.mult, ALU.add)
                    P.tt("dve", y1, y1, zs[:, tsl], ALU.mult)
                    P.copy("dve", K.YB.sub(c)[:, tsl], y1)
                    P.act(ec[:, 0:TW], y1, AF.Square)
                    P.mm(pstat[si * NJ + j][:, 0:TW], K.ones_b[:], ec[:, 0:TW], start=(c == 0), stop=(c == 7))
                if g == "p":
                    pst = K.PS[5]
                    for d in range(2):
                        for hh in range(2):
                            h = 2 * c + hh
                            for i in range(2):
                                ck = si * 2 + i
                                bs = Pb[i % 2]
                                P.ts("dve", bs[:, 0:128], Btok[:, ck, :], wtok[:, ck, d * 32 + h:d * 32 + h + 1], None, ALU.mult)
                                P.mm(pst[:, (d * 2 + hh) * 64:(d * 2 + hh + 1) * 64], bs[:, 0:128], xtok[:, ck, hh * 64:(hh + 1) * 64],
                                     start=(i == 0), stop=(i == 1))
                    P.copy("act", ytmp[:, 0:256], pst[:, 0:256])
                    for d in range(2):
                        P.dma("sp", di["nssd"][d, si, l, 2 * c:2 * c + 2].rearrange("h n p -> n h p"),
                              ytmp[:, d * 128:(d + 1) * 128].rearrange("p (a b) -> p a b", b=64), out_dram=True)
    for si, (s0, Ls) in enumerate(seqs):
        for j in range(NJ):
            tsl = slice(s0 + j * TW, s0 + (j + 1) * TW)
            r_ = ytmp[:, 0:TW]
            P.act(r_, pstat[si * NJ + j][:, 0:TW], AF.Ln, scale=1.0 / 1024, bias=EPS)
            P.act(r_, r_, AF.Exp, scale=-0.5)
            for c in range(8):
                P.stt(K.YB.sub(c)[:, tsl], K.YB.sub(c)[:, tsl], pvv(K, "ssd_ng", c), r_, ALU.mult, ALU.mult)


BRANCHES[2] = br_ssd
BRANCH_ORDER = [2, 0, 1, 3]


_NC_CACHE = {}
STAGE = 99
SUBSTAGE = 9
BR = (0, 1, 2, 3)
NCORES = 8


def _fm(v, nch):
    return np.ascontiguousarray(np.asarray(v, np.float32).reshape(nch, 128).T)


def _pack_pv(inp, l):
    pv = np.zeros((128, NPV), np.float32)

    def put(name, arr):
        o, w = PV_OFF[name]
        arr = np.asarray(arr, np.float32).reshape(128, w)
        pv[:, o:o + w] = arr
    put("n1g", _fm(inp["norm1_g"][l], 16))
    put("n2g", _fm(inp["norm2_g"][l], 16))
    put("bgate", _fm(inp["b_gate"][l], 64))
    put("lru_cw", inp["lru_conv_w"][l].reshape(4, 8, 128).transpose(2, 0, 1))
    put("lru_cb", _fm(inp["lru_conv_b"][l], 8))
    put("lru_ba", inp["lru_ba"][l].reshape(2, 8, 128).transpose(2, 0, 1))
    put("lru_bx", inp["lru_bx"][l].reshape(2, 8, 128).transpose(2, 0, 1))
    put("lru_lam", inp["lru_lambda"][l].reshape(2, 8, 128).transpose(2, 0, 1))
    put("ret_gn", _fm(inp["ret_gn_g"][l], 8))
    put("ssd_cw", inp["ssd_conv_w"][l].reshape(4, 12, 128).transpose(2, 0, 1))
    put("ssd_cb", _fm(inp["ssd_conv_b"][l], 12))
    put("ssd_ng", _fm(inp["ssd_norm_g"][l], 8))
    put("ssd_dd", _fm(np.repeat(inp["ssd_d"][l], 64), 8))
    put("ffn_cw", inp["ffn_conv_w"][l].reshape(3, 44, 128).transpose(2, 0, 1))
    put("ffn_cb", _fm(inp["ffn_conv_b"][l], 44))
    col = np.zeros(128, np.float32)
    col[0:16] = inp["ssd_dt_bias"][l][0]
    col[32:48] = inp["ssd_dt_bias"][l][1]
    put("ssd_dtb", col)
    col = np.zeros(128, np.float32)
    col[0:16] = inp["ssd_a_log"][l][0]
    col[32:48] = inp["ssd_a_log"][l][1]
    put("ssd_alog", col)
    return pv


def _const_tables():
    t = {}
    t["ident"] = np.eye(128, dtype=np.float32)
    hh = np.arange(4, dtype=np.float64)
    gf = 1.0 - 2.0 ** (-5.0 - hh)
    gb = 1.0 - 2.0 ** (-5.5 - hh)
    p = np.arange(128)[:, None]
    m = np.arange(1920)[None, :]
    dlt = (m - p - 896).astype(np.float64)
    rm = np.zeros((4, 128, 1920), np.float64)
    for h in range(4):
        rm[h] = np.where(dlt > 0, gf[h] ** np.maximum(dlt, 0), 0.0) + np.where(dlt < 0, gb[h] ** np.maximum(-dlt, 0), 0.0) \
            + np.where(dlt == 0, 2.0, 0.0)
    t["ret_mask"] = (rm * 128 ** -0.5).astype(np.float32)
    tt = np.arange(1024, dtype=np.float64)
    rows = np.zeros((8, 1024), np.float64)
    for h in range(4):
        rows[h] = gf[h] ** (tt + 1)
        rows[4 + h] = gb[h] ** (1024 - tt)
    t["ret_rows"] = np.ascontiguousarray(np.broadcast_to(rows[None], (128, 8, 1024))).astype(np.float32)
    tail = np.zeros((128, 2, 2, 4), np.float64)
    for ch in range(2):
        s = ch * 128 + np.arange(128)
        for h in range(4):
            tail[:, ch, 0, h] = gf[h] ** (255 - s)
            tail[:, ch, 1, h] = gb[h] ** s
    t["ret_tail"] = (tail * 128 ** -0.5).reshape(128, 16).astype(np.float32)
    qc = np.arange(64)
    cs = np.clip(qc - 8, 0, 48)
    kc = np.arange(64)[:, None]
    cm = ((kc >= cs[None, :]) & (kc < cs[None, :] + 16)).astype(np.float32)
    t["na_colmask"] = np.concatenate([cm, cm], 0)
    inv = 10000.0 ** (-np.arange(32, dtype=np.float32) / 32)
    tok = np.arange(1024)
    rc = np.zeros((1024, 2, 2, 32), np.float32)
    for hf, pos in enumerate([tok // 64, tok % 64]):
        ang = pos.astype(np.float32)[:, None] * inv[None, :]
        rc[:, 0, hf] = np.cos(ang)
        rc[:, 1, hf] = np.sin(ang)
    t["rope_cs"] = np.ascontiguousarray(rc.reshape(8, 128, 2, 2, 32).transpose(1, 0, 2, 3, 4))
    m = np.arange(896)[None, :]
    p = np.arange(128)[:, None]
    sc = np.zeros((128, 2, 896), np.float32)
    sc[:, 0] = np.where(m - p >= 384, 0.0, -30000.0)
    sc[:, 1] = np.where(m - p <= 384, 0.0, -30000.0)
    t["ssd_causal"] = sc
    return t


def kernel(**inp):
    inp = {k: np.asarray(v) for k, v in inp.items()}
    n = NCORES
    if "nc" not in _NC_CACHE:
        _NC_CACHE["nc"] = build_program(STAGE, SUBSTAGE, BR)
    nc = _NC_CACHE["nc"]
    consts = _const_tables()
    shared = dict(consts)
    for k in ("w_ada", "w_in", "w_gate", "w_branch", "w_out", "ffn_w_up", "ffn_w_down", "lru_wa", "lru_wx"):
        shared[k] = np.ascontiguousarray(inp[k], dtype=np.float32)
    shared["b_adaT"] = np.stack([_fm(inp["b_ada"][l], 96) for l in range(2)])
    shared["pv"] = np.stack([_pack_pv(inp, l) for l in range(2)])
    nag = np.zeros((2, 128, 2, 128), np.float32)
    for l in range(2):
        nag[l, :, 0, :] = inp["na_q_g"][l][None, :]
        nag[l, :, 1, :] = inp["na_k_g"][l][None, :]
    shared["nag"] = nag
    rp = np.zeros((2, 8, 15, 128), np.float32)
    rp[:, :, :, 48:79] = inp["na_rpb"]
    shared["rpbp"] = rp
    in_maps = []
    for c in range(n):
        m = dict(shared)
        m["xp"] = np.ascontiguousarray(inp["x_prompt"][2 * c:2 * c + 2].reshape(512, D))
        m["xs"] = np.ascontiguousarray(inp["x_sample"][c])
        cond = np.stack([inp["c_ctx"], inp["c"][c]], 0)
        m["condT"] = np.ascontiguousarray(cond.reshape(2, 16, 128).transpose(2, 1, 0))
        h0 = np.zeros((128, 2, 2, 8), np.float32)
        for l in range(2):
            h0[:, 0, l, :] = inp["state_lru_f"][c, l].reshape(8, 128).T
            h0[:, 1, l, :] = inp["state_lru_b"][c, l].reshape(8, 128).T
        m["lru_h0"] = h0.reshape(128, 32)
        m["cache_k"] = np.ascontiguousarray(inp["cache_na_k"][c].reshape(2, 256, 1024))
        m["cache_v"] = np.ascontiguousarray(inp["cache_na_v"][c].reshape(2, 256, 1024))
        m["ret_s0"] = np.ascontiguousarray(np.stack([inp["state_ret_f"][c], inp["state_ret_b"][c]], 0))
        m["ssd_s0"] = np.ascontiguousarray(np.stack([inp["state_ssd_f"][c], inp["state_ssd_b"][c]], 0))
        in_maps.append(m)
    res = run_bass_kernel_spmd(nc, in_maps, core_ids=list(range(n)))
    R = res.results
    if n < 8:
        R = list(R) + [R[0]] * (8 - n)
    n = 8
    y_prompt = np.concatenate([R[c]["yp"].reshape(2, 256, D) for c in range(n)], 0)
    y_sample = np.stack([R[c]["ys"] for c in range(n)], 0)
    nk = np.concatenate([R[c]["nk"].reshape(2, 2, 256, 8, 128) for c in range(n)], 0)
    nv = np.concatenate([R[c]["nv"].reshape(2, 2, 256, 8, 128) for c in range(n)], 0)
    lru = np.stack([R[c]["nlru"].reshape(128, 2, 2, 2, 8) for c in range(n)], 0)
    lru = lru.transpose(2, 0, 3, 4, 5, 1).reshape(2, 16, 2, 1024)
    nret = np.stack([R[c]["nret"] for c in range(n)], 0)
    nret = nret.transpose(1, 0, 2, 3, 4, 5, 6).reshape(2, 16, 2, 4, 128, 256)
    nssd = np.stack([R[c]["nssd"] for c in range(n)], 0)
    nssd = nssd.transpose(1, 0, 2, 3, 4, 5, 6).reshape(2, 16, 2, 16, 128, 64)
    f32 = lambda a: np.ascontiguousarray(a, dtype=np.float32)
    return (f32(y_prompt), f32(y_sample), f32(nk), f32(nv), f32(lru[0]), f32(lru[1]),
            f32(nret[0]), f32(nret[1]), f32(nssd[0]), f32(nssd[1]))
```

```python
import numpy as np
import math
import concourse.bass as bass
import concourse.mybir as mybir
from concourse.bass_utils import run_bass_kernel_spmd

F32 = mybir.dt.float32
BF16 = mybir.dt.bfloat16
I32 = mybir.dt.int32
AF = mybir.ActivationFunctionType
ALU = mybir.AluOpType
AX = mybir.AxisListType

ENGS = ("pe", "act", "dve", "pool", "sp")
SEM_LIMIT = 30000


class Buf:
    __slots__ = ("name", "lw", "rd", "excl")

    def __init__(self, name):
        self.name = name
        self.lw = None
        self.rd = {}
        self.excl = False


class V:
    __slots__ = ("ap", "bufs")

    def __init__(self, ap, bufs):
        self.ap = ap
        self.bufs = bufs

    def __getitem__(self, idx):
        return V(self.ap[idx], self.bufs)

    def bitcast(self, dt):
        return V(self.ap.bitcast(dt), self.bufs)

    def rearrange(self, pat, **kw):
        return V(self.ap.rearrange(pat, **kw), self.bufs)

    def unsqueeze(self, ax):
        return V(self.ap.unsqueeze(ax), self.bufs)

    def to_broadcast(self, shape):
        return V(self.ap.to_broadcast(list(shape)), self.bufs)


class T:
    def __init__(self, h, name, nsub=1):
        self.h = h
        self.name = name
        self.bufs = [Buf("%s.%d" % (name, i)) for i in range(nsub)]

    def __getitem__(self, idx):
        return V(self.h[idx], self.bufs)

    def sub(self, i):
        return V(self.h[:, i], [self.bufs[i]])

    def subs(self, i0, i1):
        return V(self.h[:, i0:i1], self.bufs[i0:i1])


class Op:
    __slots__ = ("eng", "idx", "fn", "deps", "is_dma", "sem", "val", "signal", "waits", "vc", "gidx")


class Prog:
    def __init__(self, nc, n_dma_sems=8):
        self.nc = nc
        self.ops = {e: [] for e in ENGS}
        self.order = []
        self.n_dma_sems = n_dma_sems
        self.dma_count = {e: 0 for e in ENGS}
        self.out_bufs = []
        self.last_real = {}
        self.stack = None
        self.all_bufs = []

    def sb(self, name, shape, dtype, nsub=1):
        h = self.nc.alloc_sbuf_tensor(name, list(shape), dtype)
        t = T(h, name, nsub)
        self.all_bufs.extend(t.bufs)
        return t

    def sb_at(self, name, shape, dtype, offset, nsub=1):
        h = self.nc.alloc_sbuf_tensor_at(name, list(shape), dtype, offset=offset)
        t = T(h, name, nsub)
        self.all_bufs.extend(t.bufs)
        return t

    def ps(self, name, shape, dtype=F32):
        h = self.nc.alloc_psum_tensor(name, list(shape), dtype)
        t = T(h, name, 1)
        for b in t.bufs:
            b.excl = True
        self.all_bufs.extend(t.bufs)
        return t

    def emit(self, eng, fn, reads=(), writes=(), dma=False):
        op = Op()
        op.eng = eng
        op.idx = len(self.ops[eng])
        op.fn = fn
        op.is_dma = dma
        op.signal = False
        op.sem = None
        op.val = None
        op.waits = None
        op.vc = None
        op.gidx = len(self.order)
        deps = {}
        rb = [b for v in reads for b in v.bufs]
        wb = [b for v in writes for b in v.bufs]
        wb = wb + [b for b in rb if b.excl]
        rb = [b for b in rb if not b.excl]
        for b in rb:
            if b.lw is not None:
                deps[id(b.lw)] = b.lw
        for b in wb:
            if b.lw is not None:
                deps[id(b.lw)] = b.lw
            for r in b.rd.values():
                deps[id(r)] = r
        deps.pop(id(op), None)
        op.deps = list(deps.values())
        for b in rb:
            key = ("dma", op.gidx) if dma else eng
            b.rd[key] = op
        for b in wb:
            b.lw = op
            b.rd = {}
        self.ops[eng].append(op)
        self.order.append(op)
        if fn is not None and not dma:
            self.last_real[eng] = op
        return op

    def barrier(self):
        last = list(self.last_real.values())
        dmas = list(getattr(self, "_dma_since_barrier", []))
        for e in ENGS:
            op = self.emit(e, None, (), ())
            op.deps = last + dmas
        self._dma_since_barrier = []

    def _recent_dmas(self):
        return list(getattr(self, "_dma_since_barrier", []))

    def track_dma(self, op):
        if not hasattr(self, "_dma_since_barrier"):
            self._dma_since_barrier = []
        self._dma_since_barrier.append(op)

    @staticmethod
    def _need_wait(op, d):
        if d.fn is None:
            return False
        if d.eng != op.eng:
            return True
        if d.eng in ("pe", "sp"):
            return False
        if op.is_dma:
            return True
        return (op.idx - d.idx) <= 2

    def finalize(self):
        nc = self.nc
        dma_sem_of = {}
        self.dma_sems = {}
        per_sem_count = {}
        for op in self.order:
            if op.is_dma:
                k = self.dma_count[op.eng]
                self.dma_count[op.eng] += 1
                s = (op.eng, k % self.n_dma_sems)
                per_sem_count[s] = per_sem_count.get(s, 0) + 1
                op.sem = ("dma",) + s
                op.val = 16 * per_sem_count[s]
                op.signal = True
        for op in self.order:
            for d in op.deps:
                if d.is_dma:
                    continue
                if self._need_wait(op, d):
                    d.signal = True
        cum = {e: 0 for e in ENGS}
        for e in ENGS:
            for op in self.ops[e]:
                if op.is_dma:
                    continue
                if op.signal:
                    cum[e] += 1
                    op.sem = ("eng", e, (cum[e] - 1) // SEM_LIMIT)
                    op.val = (cum[e] - 1) % SEM_LIMIT + 1
        self.cum = cum
        known = {e: {} for e in ENGS}
        lastsig = {e: None for e in ENGS}
        prev_dma_on_sem = {}
        for op in self.order:
            e = op.eng
            kn = known[e]
            waits = []

            def need(sem, val, vc):
                if kn.get(sem, 0) >= val:
                    return
                waits.append((sem, val))
                if vc is not None:
                    for s2, v2 in vc.items():
                        if kn.get(s2, 0) < v2:
                            kn[s2] = v2
                if kn.get(sem, 0) < val:
                    kn[sem] = val

            for d in op.deps:
                if d.is_dma:
                    need(d.sem, d.val, d.vc)
                else:
                    if not self._need_wait(op, d):
                        continue
                    need(d.sem, d.val, d.vc)
            if op.is_dma:
                p = prev_dma_on_sem.get(op.sem)
                if p is not None:
                    need(p.sem, p.val, p.vc)
                prev_dma_on_sem[op.sem] = op
            op.waits = waits
            if op.is_dma:
                vc = dict(kn)
                vc[op.sem] = op.val
                op.vc = vc
            elif op.signal:
                vc = dict(kn)
                vc[op.sem] = op.val
                op.vc = vc
        return self

    def run_emit(self):
        nc = self.nc
        from contextlib import ExitStack
        sem_handles = {}
        with ExitStack() as st:
            def get_sem(key):
                if key not in sem_handles:
                    nm = "s_" + "_".join(str(x) for x in key)
                    sem_handles[key] = st.enter_context(nc.semaphore(nm))
                return sem_handles[key]

            for op in self.order:
                if op.sem is not None and op.signal:
                    get_sem(op.sem)
            block = st.enter_context(nc.Block())

            def run_engine(ename):
                def body(eng):
                    for op in self.ops[ename]:
                        for (sk, val) in op.waits:
                            eng.wait_ge(get_sem(sk), val)
                        if op.fn is None:
                            continue
                        ins = op.fn(eng)
                        if op.signal:
                            ins.then_inc(get_sem(op.sem), 16 if op.is_dma else 1)
                return body

            block.tensor(run_engine("pe"))
            block.scalar(run_engine("act"))
            block.vector(run_engine("dve"))
            block.gpsimd(run_engine("pool"))
            block.sync(run_engine("sp"))

    def dma(self, q, out, in_, out_dram=False, in_dram=False, **kw):
        reads = [] if in_dram else [in_]
        if out_dram:
            b = Buf("dramout%d" % len(self.out_bufs))
            self.out_bufs.append(b)
            wv = [V(None, [b])]
        else:
            wv = [out]
        oap = out.ap if isinstance(out, V) else out
        iap = in_.ap if isinstance(in_, V) else in_
        op = self.emit(q, lambda e: e.dma_start(out=oap, in_=iap, **kw), reads, wv, dma=True)
        self.track_dma(op)
        return op

    def final_fence(self):
        vs = [V(None, [b]) for b in self.out_bufs]
        self.emit("sp", None, vs, ())

    def mm(self, out, lhsT, rhs, start=True, stop=True, **kw):
        return self.emit("pe", lambda e: e.matmul(out.ap, lhsT.ap, rhs.ap, start=start, stop=stop, **kw),
                         [lhsT, rhs], [out])

    def transpose(self, out, in_, ident):
        return self.emit("pe", lambda e: e.transpose(out.ap, in_.ap, ident.ap), [in_, ident], [out])

    def act(self, out, in_, func, bias=None, scale=None, accum_out=None, eng="act"):
        reads = [in_]
        kw = {}
        if bias is not None:
            if isinstance(bias, V):
                reads.append(bias)
                kw["bias"] = bias.ap
            else:
                kw["bias"] = bias
        if scale is not None:
            if isinstance(scale, V):
                reads.append(scale)
                kw["scale"] = scale.ap
            else:
                kw["scale"] = scale
        writes = [out]
        if accum_out is not None:
            writes.append(accum_out)
            kw["accum_out"] = accum_out.ap
        return self.emit("act", lambda e: e.activation(out.ap, in_.ap, func, **kw), reads, writes)

    def tt(self, eng, out, in0, in1, op):
        return self.emit(eng, lambda e: e.tensor_tensor(out.ap, in0.ap, in1.ap, op), [in0, in1], [out])

    def ts(self, eng, out, in0, s1, s2, op0, op1=None):
        reads = [in0]
        a1 = s1
        a2 = s2
        if isinstance(s1, V):
            reads.append(s1)
            a1 = s1.ap
        if isinstance(s2, V):
            reads.append(s2)
            a2 = s2.ap
        if op1 is None:
            return self.emit(eng, lambda e: e.tensor_scalar(out.ap, in0.ap, a1, None, op0), reads, [out])
        return self.emit(eng, lambda e: e.tensor_scalar(out.ap, in0.ap, a1, a2, op0, op1), reads, [out])

    def stt(self, out, in0, scalar, in1, op0, op1, eng="dve"):
        reads = [in0, in1]
        a = scalar
        if isinstance(scalar, V):
            reads.append(scalar)
            a = scalar.ap
        return self.emit(eng, lambda e: e.scalar_tensor_tensor(out.ap, in0.ap, a, in1.ap, op0, op1), reads, [out])

    def copy(self, eng, out, in_):
        if eng == "act":
            return self.emit("act", lambda e: e.copy(out.ap, in_.ap), [in_], [out])
        return self.emit(eng, lambda e: e.tensor_copy(out.ap, in_.ap), [in_], [out])

    def memset(self, eng, out, val):
        return self.emit(eng, lambda e: e.memset(out.ap, val), [], [out])

    def scan(self, out, d0, d1, init, op0=None, op1=None):
        reads = [d0, d1]
        a = init
        if isinstance(init, V):
            reads.append(init)
            a = init.ap
        return self.emit("dve", lambda e: e.tensor_tensor_scan(out.ap, d0.ap, d1.ap, a, op0 or ALU.mult, op1 or ALU.add),
                         reads, [out])

    def recip(self, out, in_):
        return self.emit("dve", lambda e: e.reciprocal(out.ap, in_.ap), [in_], [out])

D = 2048
NCH = 16
EPS = 1e-6
DFF = 5632
NFF = 44
D_IN = 10784
OFF = dict(lru_x=0, lru_g=1024, ret_q=2048, ret_k=2560, ret_v=3072, ret_g=4096, ssd_z=5120,
           ssd_xbc=6144, ssd_dt=7680, na_q=7712, na_k=8736, na_v=9760)

PV_FIELDS = [("n1g", 16), ("n2g", 16), ("bgate", 64), ("lru_cw", 32), ("lru_cb", 8), ("lru_ba", 16),
             ("lru_bx", 16), ("lru_lam", 16), ("ret_gn", 8), ("ssd_cw", 48), ("ssd_cb", 12), ("ssd_ng", 8),
             ("ssd_dd", 8), ("ffn_cw", 132), ("ffn_cb", 44), ("ssd_dtb", 1), ("ssd_alog", 1)]
PV_OFF = {}
_o = 0
for _n, _w in PV_FIELDS:
    PV_OFF[_n] = (_o, _w)
    _o += _w
NPV = _o

ARENA_WORDS = 53000


class Ctx:
    pass


def build_program(stage=99, substage=9, branches=(0, 1, 2, 3)):
    nc = bass.Bass("TRN2", target_bir_lowering=False)
    P = Prog(nc)
    K = Ctx()
    K.P, K.nc, K.stage = P, nc, stage
    K.substage = substage
    K.branches = branches
    di = {}
    K.di = di

    def din(name, shape, dt=F32):
        di[name] = nc.dram_tensor(name, list(shape), dt, kind="ExternalInput").ap()

    def dout(name, shape):
        di[name] = nc.dram_tensor(name, list(shape), F32, kind="ExternalOutput").ap()

    din("xp", [512, D]); din("xs", [1024, D]); din("condT", [128, 16, 2])
    din("w_ada", [2, D, 6 * D]); din("b_adaT", [2, 128, 96]); din("pv", [2, 128, NPV])
    din("w_in", [2, D, D_IN]); din("w_gate", [2, D, 4 * D]); din("w_branch", [2, 4, 1024, D])
    din("w_out", [2, D, D]); din("ffn_w_up", [2, D, 2 * DFF]); din("ffn_w_down", [2, DFF, D])
    din("lru_wa", [2, 2, 8, 128, 128]); din("lru_wx", [2, 2, 8, 128, 128])
    din("ident", [128, 128]); din("lru_h0", [128, 32])
    din("nag", [2, 128, 2, 128]); din("cache_k", [2, 256, 1024]); din("cache_v", [2, 256, 1024])
    din("ret_s0", [2, 2, 4, 128, 256]); din("ssd_s0", [2, 2, 16, 128, 64])
    din("ret_mask", [4, 128, 1920]); din("ret_rows", [128, 8, 1024]); din("ret_tail", [128, 16])
    din("rpbp", [2, 8, 15, 128]); din("na_colmask", [128, 64]); din("rope_cs", [128, 8, 2, 2, 32])
    din("ssd_causal", [128, 2, 896])
    dout("yp", [512, D]); dout("ys", [1024, D])
    dout("nk", [2, 2, 256, 1024]); dout("nv", [2, 2, 256, 1024])
    dout("nlru", [128, 64])
    dout("nret", [2, 2, 2, 4, 128, 256]); dout("nssd", [2, 2, 2, 16, 128, 64])

    arena = nc.alloc_sbuf_tensor("arena", [128, ARENA_WORDS], F32)
    K.arena = arena

    def carve(name, off, words, dtype=F32, shape=None, nsub=1):
        ap = arena[:, off:off + words]
        if dtype == BF16:
            ap = ap.bitcast(BF16)
        if shape is not None:
            if len(shape) == 2:
                ap = ap.rearrange("p (a b) -> p a b", b=shape[1])
            elif len(shape) == 3:
                ap = ap.rearrange("p (a b c) -> p a b c", b=shape[1], c=shape[2])
        return T(ap, name, nsub)
    K.carve = carve

    K.XT = carve("XT", 0, 16384, F32, (16, 1024), nsub=16)
    K.XN = carve("XN", 16384, 8192, BF16, (16, 1024), nsub=16)
    K.WS2 = [carve("WS0", 24576, 3072, BF16), carve("WS1", 27648, 3072, BF16)]
    K.WS = K.WS2
    K.wi = 0
    K.ps_hi = 4
    A0 = 30720
    K.A0 = A0
    K.MG = carve("MG", A0, 8192, BF16, (16, 1024), nsub=16)
    K.YB = carve("YB", A0 + 8192, 4096, BF16, (8, 1024), nsub=8)
    K.HH = carve("HH", A0, 11264, BF16, (44, 512), nsub=44)
    K.HH.bufs = (K.MG.bufs + K.YB.bufs) * 2
    K.HH.bufs = [Buf("HH.%d" % i) for i in range(44)]
    C0 = 43008
    K.ident_f = carve("ident_f", C0, 128)
    K.ident_b = carve("ident_b", C0 + 128, 64, BF16)
    K.ones_f = carve("ones_f", C0 + 192, 128)
    K.ones_b = carve("ones_b", C0 + 320, 64, BF16)
    K.adaT = [carve("adaT%d" % l, C0 + 384 + l * 192, 192, F32, (96, 2)) for l in range(2)]
    K.mod = carve("mod", C0 + 768, 96, F32, (6, 16))
    K.pv = carve("pv", C0 + 864, NPV)
    K.cs = carve("cs", C0 + 864 + NPV, 16, BF16, (16, 2))
    K.stT = carve("stT", C0 + 880 + NPV, 64)
    K.lru_c1 = carve("lru_c1", C0 + 944 + NPV, 32)
    K.lru_h0 = carve("lru_h0", C0 + 976 + NPV, 32)
    assert C0 + 1008 + NPV <= 45056, (C0 + 1008 + NPV)
    K.B0 = 45056
    K.BW = ARENA_WORDS - K.B0
    K.WS3 = K.WS2 + [carve("WS2", K.B0 + 4200, 3072, BF16)]
    K.WS4 = K.WS3 + [carve("WS3", K.B0 + 1100, 3072, BF16)]
    K.PS = [P.ps("ps%d" % i, [128, 512]) for i in range(8)]
    K.psi = 0

    setup(K)
    if stage >= 1:
        run_group(K, "p")
    if stage >= 2:
        run_group(K, "s")
    P.dma("sp", di["nlru"], K.stT[:], out_dram=True)
    P.final_fence()
    P.finalize()
    P.run_emit()
    return nc


def next_ps(K, lo=0, hi=None):
    if hi is None:
        hi = K.ps_hi
    i = lo + (K.psi % (hi - lo))
    K.psi += 1
    return K.PS[i]


def pvv(K, name, j=None):
    o, w = PV_OFF[name]
    if j is None:
        return K.pv[:, o:o + w]
    return K.pv[:, o + j:o + j + 1]


def wview(slot, kc, ncols):
    ap = slot.h[:, 0:kc * ncols].rearrange("p (k n) -> p k n", n=ncols)
    return V(ap, slot.bufs)


def _wkey(src2d, nrows, ncols):
    return (src2d.tensor.name, str(src2d.offset), nrows, ncols)


def stream_w(K, src2d, nrows, ncols):
    P = K.P
    key = _wkey(src2d, nrows, ncols)
    pf = getattr(K, "pf", None)
    if pf and key in pf:
        return pf.pop(key)
    kc = nrows // 128
    assert kc * ncols <= 6144
    slot = K.WS[K.wi % len(K.WS)]
    K.wi += 1
    v = wview(slot, kc, ncols)
    P.dma("pool", v, src2d.rearrange("(k p) n -> p k n", p=128), in_dram=True)
    return v


def prefetch_w(K, src2d, nrows, ncols):
    if not hasattr(K, "pf"):
        K.pf = {}
    ws = K.WS
    K.WS = K.WS2
    v = stream_w(K, src2d, nrows, ncols)
    K.WS = ws
    K.pf[_wkey(src2d, nrows, ncols)] = v


def setup(K):
    P, di = K.P, K.di
    P.dma("sp", K.ident_f[:], di["ident"], in_dram=True)
    P.copy("dve", K.ident_b[:], K.ident_f[:])
    P.memset("dve", K.ones_f[:], 1.0)
    P.memset("dve", K.ones_b[:], 1.0)
    P.memset("dve", K.stT[:], 0.0)
    P.dma("sp", K.lru_h0[:], di["lru_h0"], in_dram=True)
    ctmp = K.carve("ctmp", K.B0, 32, F32, (16, 2))
    btmp = K.carve("btmp", K.B0 + 32, 96)
    P.dma("sp", ctmp[:], di["condT"], in_dram=True)
    P.act(K.cs[:], ctmp[:], AF.Silu)
    import os
    for l in range(2):
        if os.environ.get('NO_ADA') == '1':
            break
        psA = K.PS[4 + l]
        for blk in range(32):
            w = stream_w(K, di["w_ada"][l, :, blk * 384:(blk + 1) * 384], D, 384)
            for m in range(3):
                ch = blk * 3 + m
                for kc in range(16):
                    P.mm(psA[:, ch * 2:ch * 2 + 2], w[:, kc, m * 128:(m + 1) * 128], K.cs[:, kc, :],
                         start=(kc == 0), stop=(kc == 15))
        P.dma("sp", btmp[:], di["b_adaT"][l], in_dram=True)
        P.tt("dve", K.adaT[l][:], psA[:, 0:192].rearrange("p (a b) -> p a b", b=2),
             btmp[:].unsqueeze(2).to_broadcast([128, 96, 2]), ALU.add)
    P.barrier()


def layer_setup(K, l, g):
    P, di = K.P, K.di
    gi = 0 if g == "p" else 1
    P.dma("sp", K.pv[:], di["pv"][l], in_dram=True)
    ada = K.adaT[l]
    P.ts("dve", K.mod[:, 0, :], ada[:, 16:32, gi], 1.0, None, ALU.add)
    P.tt("dve", K.mod[:, 0, :], K.mod[:, 0, :], pvv(K, "n1g"), ALU.mult)
    P.copy("dve", K.mod[:, 1, :], ada[:, 0:16, gi])
    P.copy("dve", K.mod[:, 2, :], ada[:, 32:48, gi])
    P.ts("dve", K.mod[:, 3, :], ada[:, 64:80, gi], 1.0, None, ALU.add)
    P.tt("dve", K.mod[:, 3, :], K.mod[:, 3, :], pvv(K, "n2g"), ALU.mult)
    P.copy("dve", K.mod[:, 4, :], ada[:, 48:64, gi])
    P.copy("dve", K.mod[:, 5, :], ada[:, 80:96, gi])


def rms_modulate(K, T_, arow, brow, scr_off):
    P = K.P
    sq = [K.carve("sq%d" % i, scr_off + i * 512, 256, BF16) for i in range(2)]
    rstd = K.carve("rstd", scr_off + 1024, 512)
    tmp = [K.carve("nt%d" % i, scr_off + 1536 + i * 512, 512) for i in range(2)]
    for tt in range(T_ // 512):
        sl = slice(tt * 512, (tt + 1) * 512)
        pst = K.PS[6]
        for c in range(16):
            s = sq[c % 2]
            P.act(s[:], K.XT.sub(c)[:, sl], AF.Square)
            P.mm(pst[:], K.ones_b[:], s[:], start=(c == 0), stop=(c == 15))
        P.act(rstd[:], pst[:], AF.Ln, scale=1.0 / D, bias=EPS)
        P.act(rstd[:], rstd[:], AF.Exp, scale=-0.5)
        for c in range(16):
            t = tmp[c % 2]
            P.tt("dve", t[:], K.XT.sub(c)[:, sl], rstd[:], ALU.mult)
            P.act(K.XN.sub(c)[:, sl], t[:], AF.Identity, scale=K.mod[:, arow, c:c + 1], bias=K.mod[:, brow, c:c + 1])


def split3(K, src, parts, tmp):
    P = K.P
    P.copy("dve", parts[0], src)
    P.tt("dve", tmp, src, parts[0], ALU.subtract)
    P.copy("dve", parts[1], tmp)
    P.tt("dve", tmp, tmp, parts[1], ALU.subtract)
    P.copy("dve", parts[2], tmp)


def load_x(K, g, T_):
    P, di = K.P, K.di
    src = di["xp"] if g == "p" else di["xs"]
    stg = K.carve("xstg", K.B0, 2048)
    tmp = K.carve("xtmp", K.B0 + 2048, 2048)
    parts = [K.carve("xpt%d" % i, K.B0 + 4096 + i * 1024, 1024, BF16) for i in range(3)]
    import os
    for tc_ in range(int(os.environ.get('NCHUNK', T_ // 128))):
        P.dma("sp", stg[:], src[tc_ * 128:(tc_ + 1) * 128, :], in_dram=True)
        LX = int(os.environ.get('LX', 9))
        if LX >= 2:
            split3(K, stg[:], [p_[:] for p_ in parts], tmp[:])
        for c4 in range(4):
            if LX < 3:
                break
            ps = next_ps(K)
            for q in range(4):
                c = c4 * 4 + q
                for i in range(3):
                    P.mm(ps[:, q * 128:(q + 1) * 128], parts[i][:, c * 128:(c + 1) * 128], K.ident_b[:],
                         start=(i == 0), stop=(i == 2))
            if LX >= 4:
                P.copy("act" if c4 % 2 else "dve", K.XT.subs(c4 * 4, c4 * 4 + 4)[:, :, tc_ * 128:(tc_ + 1) * 128],
                       ps[:].rearrange("p (a b) -> p a b", b=128))


def store_x(K, g, T_):
    P, di = K.P, K.di
    dst = di["yp"] if g == "p" else di["ys"]
    stg = [K.carve("ystg%d" % i, K.B0 + i * 2048, 2048) for i in range(2)]
    tmp = K.carve("ytmp", K.B0 + 4096, 1024)
    parts = [K.carve("ypt%d" % i, K.B0 + 5120 + i * 512, 512, BF16) for i in range(3)]
    import os
    for tc_ in range(int(os.environ.get('NCHUNK', T_ // 128))):
        s = stg[tc_ % 2]
        tsl = slice(tc_ * 128, (tc_ + 1) * 128)
        for c4 in range(4):
            ps = next_ps(K)
            for q in range(4):
                c = c4 * 4 + q
                pp = [p_[:, q * 128:(q + 1) * 128] for p_ in parts]
                split3(K, K.XT.sub(c)[:, tsl], pp, tmp[:, q * 128:(q + 1) * 128])
                for i in range(3):
                    P.mm(ps[:, q * 128:(q + 1) * 128], pp[i], K.ident_b[:], start=(i == 0), stop=(i == 2))
            P.copy("act" if c4 % 2 else "dve", s[:, c4 * 512:(c4 + 1) * 512], ps[:])
        P.dma("sp", dst[tsl, :], s[:], out_dram=True)


def merge_branch(K, l, n, T_, first, scr_off):
    P, di = K.P, K.di
    sg = [K.carve("sg%d" % i, scr_off + i * 512, 512) for i in range(2)]
    NT = T_ // 512
    k = 0
    for jb in range(8):
        wg = stream_w(K, di["w_gate"][l, :, n * D + jb * 256: n * D + (jb + 1) * 256], D, 256)
        wb = stream_w(K, di["w_branch"][l, n, :, jb * 256:(jb + 1) * 256], 1024, 256)
        for jj in range(2):
            j = jb * 2 + jj
            for tt in range(NT):
                sl = slice(tt * 512, (tt + 1) * 512)
                pg = next_ps(K)
                pp = next_ps(K)
                for kc in range(16):
                    P.mm(pg[:], wg[:, kc, jj * 128:(jj + 1) * 128], K.XN.sub(kc)[:, sl], start=(kc == 0), stop=(kc == 15))
                for kc in range(8):
                    P.mm(pp[:], wb[:, kc, jj * 128:(jj + 1) * 128], K.YB.sub(kc)[:, sl], start=(kc == 0), stop=(kc == 7))
                s = sg[k % 2]
                k += 1
                P.act(s[:], pg[:], AF.Sigmoid, bias=pvv(K, "bgate", n * 16 + j))
                if first:
                    P.tt("dve", K.MG.sub(j)[:, sl], s[:], pp[:], ALU.mult)
                else:
                    P.tt("dve", s[:], s[:], pp[:], ALU.mult)
                    P.tt("dve", K.MG.sub(j)[:, sl], K.MG.sub(j)[:, sl], s[:], ALU.add)


def out_proj(K, l, T_):
    P, di = K.P, K.di
    NT = T_ // 512
    for jb in range(8):
        w = stream_w(K, di["w_out"][l, :, jb * 256:(jb + 1) * 256], D, 256)
        for jj in range(2):
            j = jb * 2 + jj
            for tt in range(NT):
                sl = slice(tt * 512, (tt + 1) * 512)
                po = next_ps(K)
                for kc in range(16):
                    P.mm(po[:], w[:, kc, jj * 128:(jj + 1) * 128], K.MG.sub(kc)[:, sl], start=(kc == 0), stop=(kc == 15))
                P.stt(K.XT.sub(j)[:, sl], po[:], K.mod[:, 2, j:j + 1], K.XT.sub(j)[:, sl], ALU.mult, ALU.add)


def ffn(K, l, g, T_, scr_off):
    P, di = K.P, K.di
    assert g == "p" and T_ == 512
    nseg, L = 2, 256
    apad = [K.carve("apad%d" % i, scr_off + i * 516, nseg * (L + 2), F32, (nseg, L + 2)) for i in range(2)]
    acc = [K.carve("facc%d" % i, scr_off + 1032 + i * 512, 512, F32, (nseg, L)) for i in range(2)]
    sl = slice(0, 512)
    r3 = lambda v: v.rearrange("p (s l) -> p s l", s=nseg)
    for mp in range(NFF // 2):
        wa = stream_w(K, di["ffn_w_up"][l, :, mp * 256:(mp + 1) * 256], D, 256)
        wv = stream_w(K, di["ffn_w_up"][l, :, DFF + mp * 256:DFF + (mp + 1) * 256], D, 256)
        pas, pvs = [], []
        for u in range(2):
            pa = next_ps(K)
            for kc in range(16):
                P.mm(pa[:], wa[:, kc, u * 128:(u + 1) * 128], K.XN.sub(kc)[:, sl], start=(kc == 0), stop=(kc == 15))
            pas.append(pa)
        for u in range(2):
            pv_ = next_ps(K)
            for kc in range(16):
                P.mm(pv_[:], wv[:, kc, u * 128:(u + 1) * 128], K.XN.sub(kc)[:, sl], start=(kc == 0), stop=(kc == 15))
            pvs.append(pv_)
        for u in range(2):
            m = mp * 2 + u
            ap_, ac = apad[u], acc[u]
            P.copy("act", ap_[:, :, 1:L + 1], r3(pas[u][:]))
            P.memset("dve", ap_[:, :, 0:1], 0.0)
            P.memset("dve", ap_[:, :, L + 1:L + 2], 0.0)
            cw = lambda kk: pvv(K, "ffn_cw", kk * NFF + m)
            P.ts("dve", ac[:], ap_[:, :, 0:L], cw(0), pvv(K, "ffn_cb", m), ALU.mult, ALU.add)
            P.stt(ac[:], ap_[:, :, 1:L + 1], cw(1), ac[:], ALU.mult, ALU.add)
            P.stt(ac[:], ap_[:, :, 2:L + 2], cw(2), ac[:], ALU.mult, ALU.add)
            P.act(ac[:], ac[:], AF.Gelu_apprx_tanh)
            P.tt("dve", r3(K.HH.sub(m)[:, :]), ac[:], r3(pvs[u][:]), ALU.mult)
    HB = NFF // 2
    for jp in range(8):
        w0 = stream_w(K, di["ffn_w_down"][l, 0:HB * 128, jp * 256:(jp + 1) * 256], HB * 128, 256)
        w1 = stream_w(K, di["ffn_w_down"][l, HB * 128:NFF * 128, jp * 256:(jp + 1) * 256], HB * 128, 256)
        for u in range(2):
            j = jp * 2 + u
            pd = next_ps(K)
            for kc in range(NFF):
                w = w0 if kc < HB else w1
                P.mm(pd[:], w[:, kc % HB, u * 128:(u + 1) * 128], K.HH.sub(kc)[:, :], start=(kc == 0), stop=(kc == NFF - 1))
            P.stt(K.XT.sub(j)[:, sl], pd[:], K.mod[:, 5, j:j + 1], K.XT.sub(j)[:, sl], ALU.mult, ALU.add)


def ffn_sample(K, l, scr_off):
    P, di = K.P, K.di
    HB = NFF // 2
    HS = K.carve("HS", K.A0, 11264, BF16, (HB, 1024), nsub=HB)
    apad = [K.carve("sapad%d" % i, scr_off + i * 2050, 1026) for i in range(2)]
    acc = [K.carve("sfacc%d" % i, scr_off + i * 2050 + 1026, 1024) for i in range(2)]
    for half in range(2):
        for mm_ in range(HB):
            m = half * HB + mm_
            slot = K.WS[K.wi % len(K.WS)]
            K.wi += 1
            w = wview(slot, 16, 256)
            P.dma("pool", w[:, :, 0:128], di["ffn_w_up"][l, :, m * 128:(m + 1) * 128].rearrange("(k p) n -> p k n", p=128), in_dram=True)
            P.dma("pool", w[:, :, 128:256], di["ffn_w_up"][l, :, DFF + m * 128:DFF + (m + 1) * 128].rearrange("(k p) n -> p k n", p=128), in_dram=True)
            ap_, ac = apad[mm_ % 2], acc[mm_ % 2]
            pvs = []
            for tt in range(2):
                sl = slice(tt * 512, (tt + 1) * 512)
                pa = next_ps(K)
                pv_ = next_ps(K)
                for kc in range(16):
                    P.mm(pa[:], w[:, kc, 0:128], K.XN.sub(kc)[:, sl], start=(kc == 0), stop=(kc == 15))
                for kc in range(16):
                    P.mm(pv_[:], w[:, kc, 128:256], K.XN.sub(kc)[:, sl], start=(kc == 0), stop=(kc == 15))
                P.copy("act", ap_[:, 1 + tt * 512:1 + (tt + 1) * 512], pa[:])
                pvs.append(pv_)
            P.memset("dve", ap_[:, 0:1], 0.0)
            P.memset("dve", ap_[:, 1025:1026], 0.0)
            cw = lambda kk: pvv(K, "ffn_cw", kk * NFF + m)
            P.ts("dve", ac[:], ap_[:, 0:1024], cw(0), pvv(K, "ffn_cb", m), ALU.mult, ALU.add)
            P.stt(ac[:], ap_[:, 1:1025], cw(1), ac[:], ALU.mult, ALU.add)
            P.stt(ac[:], ap_[:, 2:1026], cw(2), ac[:], ALU.mult, ALU.add)
            P.act(ac[:], ac[:], AF.Gelu_apprx_tanh)
            for tt in range(2):
                sl = slice(tt * 512, (tt + 1) * 512)
                P.tt("dve", HS.sub(mm_)[:, sl], ac[:, sl], pvs[tt][:], ALU.mult)
        for jp in range(8):
            w = stream_w(K, di["ffn_w_down"][l, half * HB * 128:(half + 1) * HB * 128, jp * 256:(jp + 1) * 256], HB * 128, 256)
            for u in range(2):
                j = jp * 2 + u
                for tt in range(2):
                    sl = slice(tt * 512, (tt + 1) * 512)
                    pd = next_ps(K)
                    for kc in range(HB):
                        P.mm(pd[:], w[:, kc, u * 128:(u + 1) * 128], HS.sub(kc)[:, sl], start=(kc == 0), stop=(kc == HB - 1))
                    P.stt(K.XT.sub(j)[:, sl], pd[:], K.mod[:, 5, j:j + 1], K.XT.sub(j)[:, sl], ALU.mult, ALU.add)


def run_group(K, g):
    P = K.P
    T_ = 512 if g == "p" else 1024
    sub = K.substage
    import os
    if os.environ.get("NO_LOAD") != "1":
        load_x(K, g, T_)
        P.barrier()
    for l in range(2):
        if sub < 2:
            break
        layer_setup(K, l, g)
        rms_modulate(K, T_, 0, 1, K.B0)
        P.barrier()
        if sub < 3:
            continue
        first = True
        for n in BRANCH_ORDER:
            fn = BRANCHES[n]
            if fn is None or n not in K.branches:
                continue
            K.WS, K.ps_hi = K.WS2, 4
            fn(K, l, g, T_, first)
            di = K.di
            prefetch_w(K, di["w_gate"][l, :, n * D:n * D + 256], D, 256)
            prefetch_w(K, di["w_branch"][l, n, :, 0:256], 1024, 256)
            P.barrier()
            K.WS, K.ps_hi = K.WS4, 6
            merge_branch(K, l, n, T_, first, K.B0)
            if n == BRANCH_ORDER[-1]:
                prefetch_w(K, di["w_out"][l, :, 0:256], D, 256)
            P.barrier()
            first = False
        K.WS, K.ps_hi = K.WS4, 6
        if first:
            for j in range(16):
                P.memset("dve", K.MG.sub(j)[:, 0:T_], 0.0)
        out_proj(K, l, T_)
        P.barrier()
        K.WS, K.ps_hi = K.WS3, 6
        if sub < 4:
            continue
        rms_modulate(K, T_, 3, 4, K.B0)
        if g == "p":
            prefetch_w(K, K.di["ffn_w_up"][l, :, 0:256], D, 256)
            prefetch_w(K, K.di["ffn_w_up"][l, :, DFF:DFF + 256], D, 256)
        P.barrier()
        if sub < 5:
            continue
        if g == "p":
            ffn(K, l, g, T_, K.B0)
        else:
            ffn_sample(K, l, K.B0)
        P.barrier()
        K.WS, K.ps_hi = K.WS2, 4
    import os
    if os.environ.get("NO_STORE") != "1":
        store_x(K, g, T_)
        P.barrier()


BRANCHES = [None, None, None, None]
BRANCH_ORDER = [0, 1, 2, 3]


def seq_list(g):
    return [(0, 256), (256, 256)] if g == "p" else [(0, 1024)]


def br_lru(K, l, g, T_, first):
    P, di = K.P, K.di
    seqs = seq_list(g)
    nseq, L = len(seqs), seqs[0][1]
    NT = T_ // 512
    o = K.B0
    lxp = K.carve("lxp", o, nseq * (L + 3), F32, (nseq, L + 3)); o += 1032
    gl = K.carve("lgl", o, T_ // 2, BF16); o += 512
    xc = K.carve("lxc", o, T_, F32); o += 1024
    xcb = K.carve("lxcb", o, T_ // 2, BF16); o += 512
    a_ = K.carve("la", o, T_); o += 1024
    t1 = K.carve("lt1", o, T_); o += 1024
    i_ = K.carve("li", o, T_); o += 1024
    hf = K.carve("lhf", o, T_); o += 1024
    gw = K.carve("lgw", o, 256, BF16, (4, 128)); o += 256
    assert o <= K.B0 + K.BW
    P.act(K.lru_c1[:, 0:16], pvv(K, "lru_lam"), AF.Exp, scale=-1.0)
    P.act(K.lru_c1[:, 0:16], K.lru_c1[:, 0:16], AF.Ln, bias=1.0)
    P.ts("dve", K.lru_c1[:, 0:16], K.lru_c1[:, 0:16], -8.0, None, ALU.mult)
    P.memset("dve", lxp[:, :, 0:2], 0.0)
    P.memset("dve", lxp[:, :, L + 2:L + 3], 0.0)
    xc3 = xc[:].rearrange("p (s l) -> p s l", s=nseq)
    for c in range(8):
        u_ = c % 2
        if u_ == 0:
            wX = stream_w(K, di["w_in"][l, :, OFF["lru_x"] + c * 128:OFF["lru_x"] + (c + 2) * 128], D, 256)
            wG = stream_w(K, di["w_in"][l, :, OFF["lru_g"] + c * 128:OFF["lru_g"] + (c + 2) * 128], D, 256)
        P.dma("pool", gw[:, 0:2, :], di["lru_wa"][l, :, c].rearrange("r d e -> d r e"), in_dram=True)
        P.dma("pool", gw[:, 2:4, :], di["lru_wx"][l, :, c].rearrange("r d e -> d r e"), in_dram=True)
        for tt in range(NT):
            sl = slice(tt * 512, (tt + 1) * 512)
            px = next_ps(K)
            pg = next_ps(K)
            for kc in range(16):
                P.mm(px[:], wX[:, kc, u_ * 128:(u_ + 1) * 128], K.XN.sub(kc)[:, sl], start=(kc == 0), stop=(kc == 15))
            for kc in range(16):
                P.mm(pg[:], wG[:, kc, u_ * 128:(u_ + 1) * 128], K.XN.sub(kc)[:, sl], start=(kc == 0), stop=(kc == 15))
            if g == "p":
                P.copy("act", lxp[:, :, 2:2 + L], px[:].rearrange("p (s l) -> p s l", s=nseq))
            else:
                P.copy("act", lxp[:, 0, 2 + tt * 512:2 + (tt + 1) * 512], px[:])
            P.act(gl[:, sl], pg[:], AF.Gelu_apprx_tanh)
        cw = lambda kk: pvv(K, "lru_cw", kk * 8 + c)
        P.ts("dve", xc3, lxp[:, :, 0:L], cw(0), pvv(K, "lru_cb", c), ALU.mult, ALU.add)
        for kk in range(1, 4):
            P.stt(xc3, lxp[:, :, kk:kk + L], cw(kk), xc3, ALU.mult, ALU.add)
        P.copy("act", xcb[:], xc[:])
        for d in range(2):
            for tt in range(NT):
                sl = slice(tt * 512, (tt + 1) * 512)
                pr = next_ps(K)
                pi = next_ps(K)
                P.mm(pr[:], gw[:, d, :], xcb[:, sl], start=True, stop=True)
                P.mm(pi[:], gw[:, 2 + d, :], xcb[:, sl], start=True, stop=True)
                P.act(a_[:, sl], pr[:], AF.Sigmoid, bias=pvv(K, "lru_ba", d * 8 + c))
                P.act(i_[:, sl], pi[:], AF.Sigmoid, bias=pvv(K, "lru_bx", d * 8 + c))
            P.act(a_[:], a_[:], AF.Exp, scale=K.lru_c1[:, d * 8 + c:d * 8 + c + 1])
            P.tt("dve", t1[:], a_[:], a_[:], ALU.mult)
            P.act(t1[:], t1[:], AF.Ln, scale=-1.0, bias=1.0)
            P.act(t1[:], t1[:], AF.Exp, scale=0.5)
            P.tt("dve", i_[:], i_[:], xc[:], ALU.mult)
            P.tt("dve", i_[:], i_[:], t1[:], ALU.mult)
            hdst = hf if d == 0 else t1
            for si, (s0, Ls) in enumerate(seqs):
                if g == "p":
                    init = 0.0
                else:
                    col = (d * 2 + l) * 8 + c
                    init = K.lru_h0[:, col:col + 1]
                if d == 0:
                    P.scan(hdst[:, s0:s0 + Ls], a_[:, s0:s0 + Ls], i_[:, s0:s0 + Ls], init)
                else:
                    P.scan(hdst[:, s0:s0 + Ls][:, ::-1], a_[:, s0:s0 + Ls][:, ::-1], i_[:, s0:s0 + Ls][:, ::-1], init)
                if g == "p":
                    col = d * 32 + si * 16 + l * 8 + c
                    src_col = s0 + Ls - 1 if d == 0 else s0
                    P.copy("dve", K.stT[:, col:col + 1], hdst[:, src_col:src_col + 1])
        P.tt("dve", hf[:], hf[:], t1[:], ALU.add)
        P.tt("dve", K.YB.sub(c)[:, 0:T_], hf[:], gl[:], ALU.mult)


BRANCHES[0] = br_lru


def br_na(K, l, g, T_, first):
    P, di = K.P, K.di
    seqs = seq_list(g)
    NG = T_ // 512
    NCK = T_ // 128
    o = K.B0
    qT = K.carve("nqT", o, T_ // 2, BF16); o += 512
    kT = K.carve("nkT", o, T_ // 2, BF16); o += 512
    vh = K.carve("nvh", o, NCK * 64, BF16, (NCK, 128)); o += 512
    sq = K.carve("nsq", o, 1024, F32, (8, 128)); o += 1024
    ss = K.carve("nss", o, 8); o += 8
    qn = K.carve("nqn", o, 1024, F32, (8, 128)); o += 1024
    eeqb = K.carve("neeqb", o, 512, BF16); o += 512
    qb = V(eeqb.h.rearrange("p (a b) -> p a b", b=128), eeqb.bufs)
    gains = K.carve("ngain", o, 256, F32, (2, 128)); o += 256
    vf = None
    if g == "p":
        vf = K.carve("nvf", o, 512, F32, (4, 128)); o += 512
    ee = [eeqb[:, 0:512], eeqb[:, 512:1024]]
    rden = None
    if g == "p":
        rden = K.carve("nrden", o, 512); o += 512
    if g == "s":
        rope = K.carve("nrope", o, 512, F32, (4, 128)); o += 512
        tAB = K.carve("ntAB", o, 512); o += 512
        tA = tAB[:, 0:256]
        tB = tAB[:, 256:512]
        NS_ = Ctx()
        NS_.T3f = K.carve("nT3f", o, 512, BF16, (16, 64)); o += 512
        NS_.T3m = K.carve("nT3m", o, 512, BF16, (16, 64)); o += 512
        NS_.kctok = K.carve("nkctok", o, 128, BF16, (2, 128)); o += 128
        NS_.kcT = K.carve("nkcT", o, 128, BF16); o += 128
        NS_.vc = K.carve("nvc", o, 128, BF16, (2, 128)); o += 128
        NS_.cmask = K.carve("ncmask", o, 64); o += 64
        NS_.tAB, NS_.qn, NS_.sq = tAB, qn, sq
        P.dma("sp", NS_.cmask[:], di["na_colmask"], in_dram=True)
    assert o <= K.B0 + K.BW, (o - K.B0, K.BW)
    P.dma("sp", gains[:], di["nag"][l], in_dram=True)
    scale = 128 ** -0.5
    for h in range(8):
        slot = K.WS[K.wi % 2]
        K.wi += 1
        w = wview(slot, 16, 384)
        for i, nm in enumerate(("na_q", "na_k", "na_v")):
            c0 = OFF[nm] + h * 128
            P.dma("pool", w[:, :, i * 128:(i + 1) * 128], di["w_in"][l, :, c0:c0 + 128].rearrange("(k p) n -> p k n", p=128), in_dram=True)
        for tg in range(NG):
            pq, pk, pvv_ = K.PS[4], K.PS[5], K.PS[6]
            for tci in range(4):
                tc_ = tg * 4 + tci
                tsl = slice(tc_ * 128, (tc_ + 1) * 128)
                for i, pb in enumerate((pq, pk, pvv_)):
                    for kc in range(16):
                        P.mm(pb[:, tci * 128:(tci + 1) * 128], K.XN.sub(kc)[:, tsl], w[:, kc, i * 128:(i + 1) * 128],
                             start=(kc == 0), stop=(kc == 15))
            r3 = lambda v: v.rearrange("p (a b) -> p a b", b=128)
            P.act(sq[:, 0:4, :], r3(pq[:]), AF.Square)
            P.act(sq[:, 4:8, :], r3(pk[:]), AF.Square)
            P.emit("dve", lambda e, o_=ss[:].ap, i_=sq[:].ap: e.tensor_reduce(o_, i_, AX.X, ALU.add), [sq[:]], [ss[:]])
            P.act(ss[:], ss[:], AF.Ln, scale=1.0 / 128, bias=EPS)
            P.act(ss[:], ss[:], AF.Exp, scale=-0.5)
            for i, pb in enumerate((pq, pk)):
                P.tt("dve", qn[:, i * 4:(i + 1) * 4, :], r3(pb[:]), ss[:, i * 4:(i + 1) * 4].unsqueeze(2).to_broadcast([128, 4, 128]), ALU.mult)
                P.tt("dve", qn[:, i * 4:(i + 1) * 4, :], qn[:, i * 4:(i + 1) * 4, :],
                     gains[:, i, :].unsqueeze(1).to_broadcast([128, 4, 128]), ALU.mult)
            P.copy("act", vh[:, tg * 4:(tg + 1) * 4, :], r3(pvv_[:]))
            if g == "p":
                P.copy("dve", vf[:], r3(pvv_[:]))
                b = tg
                for si in range(2):
                    P.dma("sp", di["nk"][si, l, :, h * 128:(h + 1) * 128].rearrange("(c p) d -> p c d", p=128),
                          qn[:, 4 + si * 2:4 + si * 2 + 2, :], out_dram=True)
                    P.dma("sp", di["nv"][si, l, :, h * 128:(h + 1) * 128].rearrange("(c p) d -> p c d", p=128),
                          vf[:, si * 2:si * 2 + 2, :], out_dram=True)
                P.copy("act", qb, qn[:])
            else:
                P.dma("sp", rope[:], di["rope_cs"][:, tg * 4:(tg + 1) * 4].rearrange("p c a h f -> p c (a h f)"), in_dram=True)
                for i in range(2):
                    xv = qn[:, i * 4:(i + 1) * 4, :].rearrange("p c (h x f) -> p c h x f", h=2, x=2)
                    ov = qb[:, i * 4:(i + 1) * 4, :].rearrange("p c (h x f) -> p c h x f", h=2, x=2)
                    tb = rope[:].rearrange("p c (a h f) -> p c a h f", a=2, h=2)
                    cos, sin = tb[:, :, 0], tb[:, :, 1]
                    x1, x2 = xv[:, :, :, 0, :], xv[:, :, :, 1, :]
                    a4 = tA.rearrange("p (c h f) -> p c h f", c=4, h=2)
                    b4 = tB.rearrange("p (c h f) -> p c h f", c=4, h=2)
                    P.tt("dve", a4, x1, cos, ALU.mult)
                    P.tt("dve", b4, x2, sin, ALU.mult)
                    P.tt("dve", ov[:, :, :, 0, :], a4, b4, ALU.subtract)
                    P.tt("dve", a4, x1, sin, ALU.mult)
                    P.tt("dve", b4, x2, cos, ALU.mult)
                    P.tt("dve", ov[:, :, :, 1, :], a4, b4, ALU.add)
            pt = K.PS[7]
            ptb = V(pt.h[:].bitcast(BF16), pt.bufs)
            for i in range(2):
                for tci in range(4):
                    P.transpose(ptb[:, (i * 4 + tci) * 128:(i * 4 + tci + 1) * 128], qb[:, i * 4 + tci, :], K.ident_b[:])
            P.copy("act", qT[:, tg * 512:(tg + 1) * 512], ptb[:, 0:512])
            P.copy("dve", kT[:, tg * 512:(tg + 1) * 512], ptb[:, 512:1024])
        if g == "p":
            for si, (s0, L) in enumerate(seqs):
                po, pd = K.PS[4], K.PS[5]
                for i in range(2):
                    ck = si * 2 + i
                    pS = next_ps(K)
                    P.mm(pS[:, 0:256], kT[:, ck * 128:(ck + 1) * 128], qT[:, s0:s0 + 256], start=True, stop=True)
                    e_ = ee[i]
                    P.act(e_[:, 0:256], pS[:, 0:256], AF.Exp, scale=scale)
                    P.mm(po[:, 0:256], vh[:, ck, :], e_[:, 0:256], start=(i == 0), stop=(i == 1))
                    P.mm(pd[:, 0:256], K.ones_b[:], e_[:, 0:256], start=(i == 0), stop=(i == 1))
                P.recip(rden[:, 0:256], pd[:, 0:256])
                P.tt("dve", K.YB.sub(h)[:, s0:s0 + 256], po[:, 0:256], rden[:, 0:256], ALU.mult)
        else:
            na_sample_attn(K, l, h, qT, kT, vh, ee, rden, scale, NS_)


def na_sample_tables(K, l, h, N):
    P, di = K.P, K.di
    R = V(N.qn.h[:, 0:8, :].rearrange("p a b -> p (a b)")[:, 0:960].rearrange("p (a j) -> p a j", j=64), N.qn.bufs)
    T2 = V(N.tAB.h[:].bitcast(BF16)[:, 0:960].rearrange("p (a j) -> p a j", j=64), N.tAB.bufs)
    rp = di["rpbp"]
    base = (l * 8 + h) * 15 * 128
    src = bass.AP(rp.tensor, base, [[1, 64], [128, 15], [1, 64]])
    P.dma("sp", R[0:64], src, in_dram=True)
    P.dma("sp", R[64:128], src, in_dram=True)
    P.act(R, R, AF.Exp)
    P.tt("dve", T2, R[:, :, ::-1], N.cmask[:].unsqueeze(1).to_broadcast([128, 15, 64]), ALU.mult)
    P.memset("dve", N.T3f[:], 0.0)
    P.memset("dve", N.T3m[:], 0.0)
    P.copy("dve", N.T3f[0:64, 0:15, :], T2[0:64, ::-1, :])
    P.copy("dve", N.T3f[64:128, 1:16, :], T2[64:128, ::-1, :])
    P.copy("dve", N.T3m[0:64, 4:12, :], T2[0:64, 10:2:-1, :])
    P.copy("dve", N.T3m[64:128, 5:13, :], T2[64:128, 10:2:-1, :])
    P.dma("pool", N.kctok[:], di["cache_k"][l, :, h * 128:(h + 1) * 128].rearrange("(c p) d -> p c d", p=128), in_dram=True)
    P.dma("pool", N.vc[:], di["cache_v"][l, :, h * 128:(h + 1) * 128].rearrange("(c p) d -> p c d", p=128), in_dram=True)
    pt = K.PS[7]
    ptb = V(pt.h[:].bitcast(BF16), pt.bufs)
    for c in range(2):
        P.transpose(ptb[:, c * 128:(c + 1) * 128], N.kctok[:, c, :], K.ident_b[:])
    P.copy("act", N.kcT[:], ptb[:, 0:256])


def na_sample_attn(K, l, h, qT, kT, vh, ee, rden, scale, N):
    P = K.P
    na_sample_tables(K, l, h, N)
    Eb = [V(N.sq.h[:, 0:4, :].rearrange("p a b -> p (a b)"), N.sq.bufs), V(N.sq.h[:, 4:8, :].rearrange("p a b -> p (a b)"), N.sq.bufs)]
    for j in range(2):
        qsl = slice(j * 512, (j + 1) * 512)
        chunks = list(range(0, 6)) if j == 0 else list(range(2, 8))
        po, pd = K.PS[4], K.PS[5]
        nacc = len(chunks) + 2
        step = 0
        for i in chunks:
            pS = next_ps(K)
            P.mm(pS[:], kT[:, i * 128:(i + 1) * 128], qT[:, qsl], start=True, stop=True)
            E = Eb[step % 2]
            P.act(E, pS[:], AF.Exp, scale=scale)
            pb = ee[step % 2]
            segs = []
            for qr in range(8 * j, 8 * j + 8):
                e1 = qr - 2 * i + 7
                if qr <= 3:
                    kind = "f" if i <= 3 else "z"
                elif qr >= 13:
                    kind = "f" if i >= 4 else "z"
                else:
                    kind = "m" if 0 <= e1 <= 15 else "z"
                if segs and segs[-1][0] == kind:
                    segs[-1][2] = qr
                else:
                    segs.append([kind, qr, qr])
            for kind, qa, qb_ in segs:
                c0, c1 = (qa - 8 * j) * 64, (qb_ - 8 * j + 1) * 64
                if kind == "z":
                    P.memset("dve", pb[:, c0:c1], 0.0)
                else:
                    tab = N.T3f if kind == "f" else N.T3m
                    ea, eb = qa - 2 * i + 7, qb_ - 2 * i + 7
                    P.tt("dve", pb[:, c0:c1].rearrange("p (a b) -> p a b", b=64), E[:, c0:c1].rearrange("p (a b) -> p a b", b=64),
                         tab[:, ea:eb + 1, :], ALU.mult)
            P.mm(po[:], vh[:, i, :], pb, start=(step == 0), stop=False)
            P.mm(pd[:], K.ones_b[:], pb, start=(step == 0), stop=False)
            step += 1
        for c in range(2):
            pS = next_ps(K)
            P.mm(pS[:], N.kcT[:, c * 128:(c + 1) * 128], qT[:, qsl], start=True, stop=True)
            pb = ee[step % 2]
            P.act(pb, pS[:], AF.Exp, scale=scale)
            P.mm(po[:], N.vc[:, c, :], pb, start=False, stop=(c == 1))
            P.mm(pd[:], K.ones_b[:], pb, start=False, stop=(c == 1))
            step += 1
        P.recip(Eb[0], pd[:])
        P.tt("dve", K.YB.sub(h)[:, qsl], po[:], Eb[0], ALU.mult)


BRANCHES[3] = br_na


def br_ret(K, l, g, T_, first):
    P, di = K.P, K.di
    seqs = seq_list(g)
    NT = T_ // 512
    NCK = T_ // 128
    o = K.B0
    qT = K.carve("rqT", o, T_ // 2, BF16); o += 512
    kT = K.carve("rkT", o, T_ // 2, BF16); o += 512
    vh = K.carve("rvh", o, NCK * 128, BF16, (NCK, 256)); o += 1024
    gs = K.carve("rgs", o, 512, BF16, (2, 512)); o += 512
    mk = K.carve("rmk", o, 960, BF16); o += 960
    X = [K.carve("rX%d" % i, o + i * 256, 256, BF16) for i in range(4)]; o += 1024
    rows = K.carve("rrows", o, 512, BF16, (2, 512)); o += 512
    s0b = K.carve("rs0", o, 256, BF16, (2, 256)); o += 256
    yf = K.carve("ryf", o, 1024, F32, (2, 512)); o += 1024
    tm = [K.carve("rtm%d" % i, o + i * 512, 512) for i in range(3)]; o += 1536
    tail = K.carve("rtail", o, 16); o += 16
    assert o <= K.B0 + K.BW, (o - K.B0, K.BW)
    P.dma("sp", tail[:], di["ret_tail"], in_dram=True)
    for h in range(4):
        slot = K.WS[K.wi % 2]
        K.wi += 1
        wqk = wview(slot, 16, 256)
        for i, nm in enumerate(("ret_q", "ret_k")):
            c0 = OFF[nm] + h * 128
            P.dma("pool", wqk[:, :, i * 128:(i + 1) * 128], di["w_in"][l, :, c0:c0 + 128].rearrange("(k p) n -> p k n", p=128), in_dram=True)
        for tt in range(NT):
            sl = slice(tt * 512, (tt + 1) * 512)
            pq, pk = next_ps(K), next_ps(K)
            for kc in range(16):
                P.mm(pq[:], wqk[:, kc, 0:128], K.XN.sub(kc)[:, sl], start=(kc == 0), stop=(kc == 15))
            for kc in range(16):
                P.mm(pk[:], wqk[:, kc, 128:256], K.XN.sub(kc)[:, sl], start=(kc == 0), stop=(kc == 15))
            P.copy("act", qT[:, sl], pq[:])
            P.copy("dve", kT[:, sl], pk[:])
        wv = stream_w(K, di["w_in"][l, :, OFF["ret_v"] + h * 256:OFF["ret_v"] + (h + 1) * 256], D, 256)
        for tp in range(NCK // 2):
            pv_ = next_ps(K)
            for u in range(2):
                tc_ = tp * 2 + u
                for kc in range(16):
                    P.mm(pv_[:, u * 256:(u + 1) * 256], K.XN.sub(kc)[:, tc_ * 128:(tc_ + 1) * 128], wv[:, kc, :],
                         start=(kc == 0), stop=(kc == 15))
            P.copy("act", vh[:, tp * 2:tp * 2 + 2, :], pv_[:].rearrange("p (a b) -> p a b", b=256))
        wg = stream_w(K, di["w_in"][l, :, OFF["ret_g"] + h * 256:OFF["ret_g"] + (h + 1) * 256], D, 256)
        P.dma("pool", mk[:], di["ret_mask"][h], in_dram=True)
        if g == "s":
            P.dma("pool", s0b[:], di["ret_s0"][:, l, h].rearrange("r k v -> k r v"), in_dram=True)
        for si, (s0, L) in enumerate(seqs):
            TW = min(L, 512)
            NS = L // 128
            for j in range(L // TW):
                t0 = s0 + j * TW
                tsl = slice(t0, t0 + TW)
                for dvc in range(2):
                    pg = next_ps(K)
                    for kc in range(16):
                        P.mm(pg[:, 0:TW], wg[:, kc, dvc * 128:(dvc + 1) * 128], K.XN.sub(kc)[:, tsl], start=(kc == 0), stop=(kc == 15))
                    P.act(gs[:, dvc, 0:TW], pg[:, 0:TW], AF.Silu)
                if g == "s":
                    P.dma("pool", rows[:], di["ret_rows"][:, h:8:4, j * 512:(j + 1) * 512], in_dram=True)
                pY = [K.PS[4], K.PS[5]]
                for i in range(NS):
                    pS = next_ps(K)
                    P.mm(pS[:, 0:TW], kT[:, s0 + i * 128:s0 + (i + 1) * 128], qT[:, tsl], start=True, stop=True)
                    base = j * TW - 128 * i + 896
                    pt = X[i % 2]
                    P.tt("dve", pt[:, 0:TW], pS[:, 0:TW], mk[:, base:base + TW], ALU.mult)
                    for dvc in range(2):
                        P.mm(pY[dvc][:, 0:TW], vh[:, s0 // 128 + i, dvc * 128:(dvc + 1) * 128], pt[:, 0:TW],
                             start=(i == 0), stop=(i == NS - 1 and g == "p"))
                if g == "s":
                    for d in range(2):
                        qf = X[2 + d]
                        P.tt("dve", qf[:, 0:TW], qT[:, tsl], rows[:, d, 0:TW], ALU.mult)
                        for dvc in range(2):
                            P.mm(pY[dvc][:, 0:TW], s0b[:, d, dvc * 128:(dvc + 1) * 128], qf[:, 0:TW], start=False, stop=(d == 1))
                for dvc in range(2):
                    P.copy("act", yf[:, dvc, 0:TW], pY[dvc][:, 0:TW])
                    P.copy("dve", X[dvc][:, 0:TW], pY[dvc][:, 0:TW])
                    P.act(X[2 + dvc][:, 0:TW], pY[dvc][:, 0:TW], AF.Square)
                pm, pq2 = K.PS[6], K.PS[7]
                for dvc in range(2):
                    P.mm(pm[:, 0:TW], K.ones_b[:], X[dvc][:, 0:TW], start=(dvc == 0), stop=(dvc == 1))
                for dvc in range(2):
                    P.mm(pq2[:, 0:TW], K.ones_b[:], X[2 + dvc][:, 0:TW], start=(dvc == 0), stop=(dvc == 1))
                m_, v_, r_ = tm[0], tm[1], tm[2]
                P.act(m_[:, 0:TW], pm[:, 0:TW], AF.Identity, scale=1.0 / 256)
                P.tt("dve", v_[:, 0:TW], m_[:, 0:TW], m_[:, 0:TW], ALU.mult)
                P.stt(v_[:, 0:TW], pq2[:, 0:TW], 1.0 / 256, v_[:, 0:TW], ALU.mult, ALU.subtract)
                P.act(r_[:, 0:TW], v_[:, 0:TW], AF.Ln, bias=EPS)
                P.act(r_[:, 0:TW], r_[:, 0:TW], AF.Exp, scale=-0.5)
                for dvc in range(2):
                    y_ = yf[:, dvc, 0:TW]
                    P.tt("dve", y_, y_, m_[:, 0:TW], ALU.subtract)
                    P.tt("dve", y_, y_, r_[:, 0:TW], ALU.mult)
                    P.stt(K.YB.sub(h * 2 + dvc)[:, tsl], y_, pvv(K, "ret_gn", h * 2 + dvc), gs[:, dvc, 0:TW], ALU.mult, ALU.mult)
            if g == "p":
                pst = [K.PS[4], K.PS[5]]
                for i in range(2):
                    ptb_t = K.PS[6]
                    ptb = V(ptb_t.h[:].bitcast(BF16), ptb_t.bufs)
                    P.transpose(ptb[:, 0:128], kT[:, s0 + i * 128:s0 + (i + 1) * 128], K.ident_b[:])
                    for d in range(2):
                        ks = X[d]
                        col = (i * 2 + d) * 4 + h
                        P.ts("dve", ks[:, 0:128], ptb[:, 0:128], tail[:, col:col + 1], None, ALU.mult)
                        P.mm(pst[d][:, 0:256], ks[:, 0:128], vh[:, si * 2 + i, :], start=(i == 0), stop=(i == 1))
                for d in range(2):
                    P.copy("act" if d else "dve", yf[:, d, 0:256], pst[d][:, 0:256])
                    P.dma("sp", di["nret"][d, si, l, h], yf[:, d, 0:256], out_dram=True)


BRANCHES[1] = br_ret


def br_ssd(K, l, g, T_, first):
    P, di = K.P, K.di
    seqs = seq_list(g)
    nseq, L = len(seqs), seqs[0][1]
    NT = T_ // 512
    NCK = T_ // 128
    TW = min(L, 512)
    NS = L // 128
    NJ = L // TW
    H2 = T_ // 2
    o = K.A0
    Gsb = [K.carve("sG%d" % i, o + i * H2, H2, BF16) for i in range(NS)]; o += NS * H2
    parts = [K.carve("spart%d" % i, o + i * H2, H2, BF16) for i in range(3)]; o += 3 * H2
    biasT = K.carve("sbiasT", o, T_); o += T_
    onesL = K.carve("sones", o, H2, BF16, None, nsub=2); o += H2
    BCT = K.carve("sBCT", o, 2 * H2, BF16, (2, T_)); o += 2 * H2
    assert o <= K.A0 + 8192, o - K.A0
    o = K.B0
    cum = K.carve("scum", o, T_); o += T_
    ytmp = K.carve("sytmp", o, T_); o += T_
    btok = K.carve("sbtok", o, NCK * 64, F32, (NCK, 64)); o += NCK * 64
    if g == "p":
        wtok = K.carve("swtok", o, NCK * 64, F32, (NCK, 64)); o += NCK * 64
        Btok = K.carve("sBtok", o, NCK * 64, BF16, (NCK, 128)); o += NCK * 64
    xsT = K.carve("sxsT", o, H2, BF16); o += H2
    xtok = K.carve("sxtok", o, NCK * 64, BF16, (NCK, 128)); o += NCK * 64
    zs = K.carve("szs", o, H2, BF16); o += H2
    scrA = K.carve("sscrA", o, 2048, F32, None, nsub=6); o += 2048
    sel2 = [K.carve("ssel%d" % i, o + i * 64, 64, BF16) for i in range(2)]; o += 128
    Cf = K.carve("sCf", o, 256, BF16); o += 256
    ec = K.carve("sec", o, 256, BF16); o += 256
    s0h = K.carve("ss0h", o, 128, BF16, (4, 64)); o += 128
    caus = K.carve("scaus", o, 896, BF16, (2, 896)); o += 896
    small = K.carve("ssmall", o, 8); o += 8
    assert o <= K.B0 + K.BW, (o - K.B0, K.BW)
    Mf = [V(scrA.h[:, 0:256].bitcast(BF16), [scrA.bufs[0]]), V(scrA.h[:, 256:512].bitcast(BF16), [scrA.bufs[1]])]
    Mb = [V(scrA.h[:, 512:768].bitcast(BF16), [scrA.bufs[2]]), V(scrA.h[:, 768:1024].bitcast(BF16), [scrA.bufs[3]])]
    tmpc = [V(scrA.h[:, 1024:1536], [scrA.bufs[4]]), V(scrA.h[:, 1536:2048], [scrA.bufs[5]])]
    Pb = [V(onesL.h[:, 0:TW], [onesL.bufs[0]]), V(onesL.h[:, TW:2 * TW], [onesL.bufs[1]])]
    cpad = V(scrA.h[:, 0:nseq * (L + 3)].rearrange("p (s l) -> p s l", s=nseq), scrA.bufs)
    dtT = ytmp
    P.dma("pool", caus[:], di["ssd_causal"], in_dram=True)
    P.memset("dve", onesL[:], 1.0)
    P.memset("dve", cpad[:, :, 0:2], 0.0)
    P.memset("dve", cpad[:, :, L + 2:L + 3], 0.0)

    def conv_chunk(cidx, dst_bf):
        c0 = OFF["ssd_xbc"] + cidx * 128
        w = stream_w(K, di["w_in"][l, :, c0:c0 + 128], D, 128)
        P.memset("dve", cpad[:, :, 0:2], 0.0)
        P.memset("dve", cpad[:, :, L + 2:L + 3], 0.0)
        for tt in range(NT):
            sl = slice(tt * 512, (tt + 1) * 512)
            px = next_ps(K)
            for kc in range(16):
                P.mm(px[:], w[:, kc, :], K.XN.sub(kc)[:, sl], start=(kc == 0), stop=(kc == 15))
            if g == "p":
                P.copy("act", cpad[:, :, 2:2 + L], px[:].rearrange("p (s l) -> p s l", s=nseq))
            else:
                P.copy("act", cpad[:, 0, 2 + tt * 512:2 + (tt + 1) * 512], px[:])
        acc = ytmp[:].rearrange("p (s l) -> p s l", s=nseq)
        cw = lambda kk: pvv(K, "ssd_cw", kk * 12 + cidx)
        P.ts("dve", acc, cpad[:, :, 0:L], cw(0), pvv(K, "ssd_cb", cidx), ALU.mult, ALU.add)
        for kk in range(1, 4):
            P.stt(acc, cpad[:, :, kk:kk + L], cw(kk), acc, ALU.mult, ALU.add)
        P.act(dst_bf, ytmp[:], AF.Silu)

    def split_T(src64, cols):
        split3(K, src64, [p_[0:64, 0:cols] for p_ in parts], biasT[0:64, 0:cols] if False else tmp64[0:64, 0:cols])

    tmp64 = K.carve("stmp64", K.A0 + NS * H2 + 3 * H2, T_)
    tmp64 = biasT

    wdt = stream_w(K, di["w_in"][l, :, OFF["ssd_dt"]:OFF["ssd_dt"] + 32], D, 32)
    P.memset("dve", dtT[0:64, :], 1.718281828)
    for tt in range(NT):
        sl = slice(tt * 512, (tt + 1) * 512)
        pdt = next_ps(K)
        for d in range(2):
            for kc in range(16):
                P.mm(pdt[d * 32:d * 32 + 16, :], wdt[:, kc, d * 16:(d + 1) * 16], K.XN.sub(kc)[:, sl], start=(kc == 0), stop=(kc == 15))
        for d in range(2):
            P.act(dtT[d * 32:d * 32 + 16, sl], pdt[d * 32:d * 32 + 16, :], AF.Exp, bias=pvv(K, "ssd_dtb")[d * 32:d * 32 + 16])
    P.act(dtT[0:64, :], dtT[0:64, :], AF.Ln, bias=1.0)
    P.act(small[0:64, 0:1], pvv(K, "ssd_alog")[0:64], AF.Exp)
    P.ts("dve", small[0:64, 0:1], small[0:64, 0:1], -1.0, None, ALU.mult)
    P.ts("dve", biasT[0:64, :], dtT[0:64, :], small[0:64, 0:1], None, ALU.mult)
    P.memset("dve", cum[0:64, :], 0.0)
    for (s0, Ls) in seqs:
        P.scan(cum[0:16, s0:s0 + Ls], onesL[0:16, s0:s0 + Ls], biasT[0:16, s0:s0 + Ls], 0.0)
        P.scan(cum[32:48, s0:s0 + Ls][:, ::-1], onesL[32:48, s0:s0 + Ls][:, ::-1], biasT[32:48, s0:s0 + Ls][:, ::-1], 0.0)
    P.act(dtT[0:64, :], dtT[0:64, :], AF.Ln)
    P.tt("dve", biasT[0:64, :], dtT[0:64, :], cum[0:64, :], ALU.subtract)

    def to_tok(dst):
        for c4 in range(0, NCK, 8):
            pb_ = next_ps(K)
            n = min(8, NCK - c4)
            for u in range(n):
                ck = c4 + u
                for i3 in range(3):
                    P.mm(pb_[:, u * 64:(u + 1) * 64], parts[i3][0:64, ck * 128:(ck + 1) * 128], K.ident_b[0:64, 0:64],
                         start=(i3 == 0), stop=(i3 == 2))
            P.copy("act", dst[:, c4:c4 + n, :], pb_[:, 0:n * 64].rearrange("p (a b) -> p a b", b=64))

    split3(K, biasT[0:64, :], [p_[0:64, :] for p_ in parts], dtT[0:64, :])
    to_tok(btok)
    if g == "p":
        for (s0, Ls) in seqs:
            P.act(biasT[0:16, s0:s0 + Ls], biasT[0:16, s0:s0 + Ls], AF.Exp, bias=cum[0:16, s0 + Ls - 1:s0 + Ls])
            P.act(biasT[32:48, s0:s0 + Ls], biasT[32:48, s0:s0 + Ls], AF.Exp, bias=cum[32:48, s0:s0 + 1])
        split3(K, biasT[0:64, :], [p_[0:64, :] for p_ in parts], dtT[0:64, :])
        to_tok(wtok)
    split3(K, cum[0:64, :], [p_[0:64, :] for p_ in parts], dtT[0:64, :])

    NSTAT = nseq * NJ
    pstat = [K.PS[6], K.PS[7]]
    assert NSTAT == 2
    for grp in range(2):
        conv_chunk(8 + grp, BCT[:, 0, :])
        conv_chunk(10 + grp, BCT[:, 1, :])
        BT, CT = BCT[:, 0, :], BCT[:, 1, :]
        for si, (s0, Ls) in enumerate(seqs):
            for i in range(NS):
                for j in range(NJ):
                    pG = next_ps(K)
                    P.mm(pG[:, 0:TW], BT[:, s0 + i * 128:s0 + (i + 1) * 128], CT[:, s0 + j * TW:s0 + (j + 1) * TW], start=True, stop=True)
                    P.copy("act" if (i + j) % 2 else "dve", Gsb[i][:, s0 + j * TW:s0 + (j + 1) * TW], pG[:, 0:TW])
        if g == "p":
            ptt = K.PS[5]
            ptb = V(ptt.h[:].bitcast(BF16), ptt.bufs)
            for ck in range(NCK):
                P.transpose(ptb[:, ck * 128:(ck + 1) * 128], BT[:, ck * 128:(ck + 1) * 128], K.ident_b[:])
            P.copy("act", Btok[:], ptb[:, 0:NCK * 128].rearrange("p (a b) -> p a b", b=128))
        for cp in range(4):
            c = grp * 4 + cp
            conv_chunk(c, xsT[:])
            ptt = K.PS[5]
            ptb = V(ptt.h[:].bitcast(BF16), ptt.bufs)
            for ck in range(NCK):
                P.transpose(ptb[:, ck * 128:(ck + 1) * 128], xsT[:, ck * 128:(ck + 1) * 128], K.ident_b[:])
            P.copy("act", xtok[:], ptb[:, 0:NCK * 128].rearrange("p (a b) -> p a b", b=128))
            wz = stream_w(K, di["w_in"][l, :, OFF["ssd_z"] + c * 128:OFF["ssd_z"] + (c + 1) * 128], D, 128)
            for tt in range(NT):
                sl = slice(tt * 512, (tt + 1) * 512)
                pz = next_ps(K)
                for kc in range(16):
                    P.mm(pz[:], wz[:, kc, :], K.XN.sub(kc)[:, sl], start=(kc == 0), stop=(kc == 15))
                P.act(zs[:, sl], pz[:], AF.Silu)
            if g == "s":
                for d in range(2):
                    P.dma("pool", s0h[:, d * 2:d * 2 + 2, :], di["ssd_s0"][d, l, 2 * c:2 * c + 2].rearrange("h n p -> n h p"), in_dram=True)
            for si, (s0, Ls) in enumerate(seqs):
                for j in range(NJ):
                    t0 = s0 + j * TW
                    tsl = slice(t0, t0 + TW)
                    pY = K.PS[4]
                    for hh in range(2):
                        h = 2 * c + hh
                        yv = pY[hh * 64:(hh + 1) * 64, 0:TW]
                        pc = [next_ps(K), next_ps(K)]
                        for d in range(2):
                            row = d * 32 + h
                            sel = sel2[d]
                            P.copy("dve", sel[0:64, :], K.ident_b[0:64, row:row + 1].to_broadcast([64, 128]))
                            for i3 in range(3):
                                P.mm(pc[d][:, 0:TW], sel[0:64, :], parts[i3][0:64, tsl], start=(i3 == 0), stop=(i3 == 2))
                        for i in range(NS):
                            d0 = 128 * i - j * TW
                            need_f = d0 <= TW - 1
                            full_f = d0 + 127 <= 0
                            need_b = d0 + 127 >= 0
                            full_b = d0 >= TW - 1
                            off = 384 - d0
                            ck = s0 // 128 + i
                            ms = []
                            for d, need, full, Mx in ((0, need_f, full_f, Mf[i % 2]), (1, need_b, full_b, Mb[i % 2])):
                                if not need:
                                    continue
                                bcol = btok[:, ck, d * 32 + h:d * 32 + h + 1]
                                if full:
                                    P.act(Mx[:, 0:TW], pc[d][:, 0:TW], AF.Exp, bias=bcol)
                                else:
                                    tq = tmpc[d]
                                    P.tt("dve", tq[:, 0:TW], pc[d][:, 0:TW], caus[:, d, off:off + TW], ALU.add)
                                    P.act(Mx[:, 0:TW], tq[:, 0:TW], AF.Exp, bias=bcol)
                                ms.append(Mx)
                            pb_ = Pb[i % 2]
                            if len(ms) == 2:
                                P.tt("dve", ms[0][:, 0:TW], ms[0][:, 0:TW], ms[1][:, 0:TW], ALU.add)
                            P.tt("dve", pb_[:, 0:TW], Gsb[i][:, tsl], ms[0][:, 0:TW], ALU.mult)
                            P.mm(yv, xtok[:, ck, hh * 64:(hh + 1) * 64], pb_[:, 0:TW], start=(i == 0), stop=(i == NS - 1 and g == "p"))
                        if g == "s":
                            for d in range(2):
                                P.act(ec[:, 0:TW], pc[d][:, 0:TW], AF.Exp)
                                P.tt("dve", Cf[:, 0:TW], CT[:, tsl], ec[:, 0:TW], ALU.mult)
                                P.mm(yv, s0h[:, d * 2 + hh, :], Cf[:, 0:TW], start=False, stop=(d == 1))
                    y1 = ytmp[:, 0:TW]
                    P.stt(y1, xsT[:, tsl], pvv(K, "ssd_dd", c), pY[:, 0:TW], ALU.mult, ALU.add)
                    P.tt("dve", y1, y1, zs[:, tsl], ALU.mult)
                    P.copy("dve", K.YB.sub(c)[:, tsl], y1)
                    P.act(ec[:, 0:TW], y1, AF.Square)
                    P.mm(pstat[si * NJ + j][:, 0:TW], K.ones_b[:], ec[:, 0:TW], start=(c == 0), stop=(c == 7))
                if g == "p":
                    pst = K.PS[5]
                    for d in range(2):
                        for hh in range(2):
                            h = 2 * c + hh
                            for i in range(2):
                                ck = si * 2 + i
                                bs = Pb[i % 2]
                                P.ts("dve", bs[:, 0:128], Btok[:, ck, :], wtok[:, ck, d * 32 + h:d * 32 + h + 1], None, ALU.mult)
                                P.mm(pst[:, (d * 2 + hh) * 64:(d * 2 + hh + 1) * 64], bs[:, 0:128], xtok[:, ck, hh * 64:(hh + 1) * 64],
                                     start=(i == 0), stop=(i == 1))
                    P.copy("act", ytmp[:, 0:256], pst[:, 0:256])
                    for d in range(2):
                        P.dma("sp", di["nssd"][d, si, l, 2 * c:2 * c + 2].rearrange("h n p -> n h p"),
                              ytmp[:, d * 128:(d + 1) * 128].rearrange("p (a b) -> p a b", b=64), out_dram=True)
    for si, (s0, Ls) in enumerate(seqs):
        for j in range(NJ):
            tsl = slice(s0 + j * TW, s0 + (j + 1) * TW)
            r_ = ytmp[:, 0:TW]
            P.act(r_, pstat[si * NJ + j][:, 0:TW], AF.Ln, scale=1.0 / 1024, bias=EPS)
            P.act(r_, r_, AF.Exp, scale=-0.5)
            for c in range(8):
                P.stt(K.YB.sub(c)[:, tsl], K.YB.sub(c)[:, tsl], pvv(K, "ssd_ng", c), r_, ALU.mult, ALU.mult)


BRANCHES[2] = br_ssd
BRANCH_ORDER = [2, 0, 1, 3]


_NC_CACHE = {}
STAGE = 99
SUBSTAGE = 9
BR = (0, 1, 2, 3)
NCORES = 8


def _fm(v, nch):
    return np.ascontiguousarray(np.asarray(v, np.float32).reshape(nch, 128).T)


def _pack_pv(inp, l):
    pv = np.zeros((128, NPV), np.float32)

    def put(name, arr):
        o, w = PV_OFF[name]
        arr = np.asarray(arr, np.float32).reshape(128, w)
        pv[:, o:o + w] = arr
    put("n1g", _fm(inp["norm1_g"][l], 16))
    put("n2g", _fm(inp["norm2_g"][l], 16))
    put("bgate", _fm(inp["b_gate"][l], 64))
    put("lru_cw", inp["lru_conv_w"][l].reshape(4, 8, 128).transpose(2, 0, 1))
    put("lru_cb", _fm(inp["lru_conv_b"][l], 8))
    put("lru_ba", inp["lru_ba"][l].reshape(2, 8, 128).transpose(2, 0, 1))
    put("lru_bx", inp["lru_bx"][l].reshape(2, 8, 128).transpose(2, 0, 1))
    put("lru_lam", inp["lru_lambda"][l].reshape(2, 8, 128).transpose(2, 0, 1))
    put("ret_gn", _fm(inp["ret_gn_g"][l], 8))
    put("ssd_cw", inp["ssd_conv_w"][l].reshape(4, 12, 128).transpose(2, 0, 1))
    put("ssd_cb", _fm(inp["ssd_conv_b"][l], 12))
    put("ssd_ng", _fm(inp["ssd_norm_g"][l], 8))
    put("ssd_dd", _fm(np.repeat(inp["ssd_d"][l], 64), 8))
    put("ffn_cw", inp["ffn_conv_w"][l].reshape(3, 44, 128).transpose(2, 0, 1))
    put("ffn_cb", _fm(inp["ffn_conv_b"][l], 44))
    col = np.zeros(128, np.float32)
    col[0:16] = inp["ssd_dt_bias"][l][0]
    col[32:48] = inp["ssd_dt_bias"][l][1]
    put("ssd_dtb", col)
    col = np.zeros(128, np.float32)
    col[0:16] = inp["ssd_a_log"][l][0]
    col[32:48] = inp["ssd_a_log"][l][1]
    put("ssd_alog", col)
    return pv


def _const_tables():
    t = {}
    t["ident"] = np.eye(128, dtype=np.float32)
    hh = np.arange(4, dtype=np.float64)
    gf = 1.0 - 2.0 ** (-5.0 - hh)
    gb = 1.0 - 2.0 ** (-5.5 - hh)
    p = np.arange(128)[:, None]
    m = np.arange(1920)[None, :]
    dlt = (m - p - 896).astype(np.float64)
    rm = np.zeros((4, 128, 1920), np.float64)
    for h in range(4):
        rm[h] = np.where(dlt > 0, gf[h] ** np.maximum(dlt, 0), 0.0) + np.where(dlt < 0, gb[h] ** np.maximum(-dlt, 0), 0.0) \
            + np.where(dlt == 0, 2.0, 0.0)
    t["ret_mask"] = (rm * 128 ** -0.5).astype(np.float32)
    tt = np.arange(1024, dtype=np.float64)
    rows = np.zeros((8, 1024), np.float64)
    for h in range(4):
        rows[h] = gf[h] ** (tt + 1)
        rows[4 + h] = gb[h] ** (1024 - tt)
    t["ret_rows"] = np.ascontiguousarray(np.broadcast_to(rows[None], (128, 8, 1024))).astype(np.float32)
    tail = np.zeros((128, 2, 2, 4), np.float64)
    for ch in range(2):
        s = ch * 128 + np.arange(128)
        for h in range(4):
            tail[:, ch, 0, h] = gf[h] ** (255 - s)
            tail[:, ch, 1, h] = gb[h] ** s
    t["ret_tail"] = (tail * 128 ** -0.5).reshape(128, 16).astype(np.float32)
    qc = np.arange(64)
    cs = np.clip(qc - 8, 0, 48)
    kc = np.arange(64)[:, None]
    cm = ((kc >= cs[None, :]) & (kc < cs[None, :] + 16)).astype(np.float32)
    t["na_colmask"] = np.concatenate([cm, cm], 0)
    inv = 10000.0 ** (-np.arange(32, dtype=np.float32) / 32)
    tok = np.arange(1024)
    rc = np.zeros((1024, 2, 2, 32), np.float32)
    for hf, pos in enumerate([tok // 64, tok % 64]):
        ang = pos.astype(np.float32)[:, None] * inv[None, :]
        rc[:, 0, hf] = np.cos(ang)
        rc[:, 1, hf] = np.sin(ang)
    t["rope_cs"] = np.ascontiguousarray(rc.reshape(8, 128, 2, 2, 32).transpose(1, 0, 2, 3, 4))
    m = np.arange(896)[None, :]
    p = np.arange(128)[:, None]
    sc = np.zeros((128, 2, 896), np.float32)
    sc[:, 0] = np.where(m - p >= 384, 0.0, -30000.0)
    sc[:, 1] = np.where(m - p <= 384, 0.0, -30000.0)
    t["ssd_causal"] = sc
    return t


def kernel(**inp):
    inp = {k: np.asarray(v) for k, v in inp.items()}
    n = NCORES
    if "nc" not in _NC_CACHE:
        _NC_CACHE["nc"] = build_program(STAGE, SUBSTAGE, BR)
    nc = _NC_CACHE["nc"]
    consts = _const_tables()
    shared = dict(consts)
    for k in ("w_ada", "w_in", "w_gate", "w_branch", "w_out", "ffn_w_up", "ffn_w_down", "lru_wa", "lru_wx"):
        shared[k] = np.ascontiguousarray(inp[k], dtype=np.float32)
    shared["b_adaT"] = np.stack([_fm(inp["b_ada"][l], 96) for l in range(2)])
    shared["pv"] = np.stack([_pack_pv(inp, l) for l in range(2)])
    nag = np.zeros((2, 128, 2, 128), np.float32)
    for l in range(2):
        nag[l, :, 0, :] = inp["na_q_g"][l][None, :]
        nag[l, :, 1, :] = inp["na_k_g"][l][None, :]
    shared["nag"] = nag
    rp = np.zeros((2, 8, 15, 128), np.float32)
    rp[:, :, :, 48:79] = inp["na_rpb"]
    shared["rpbp"] = rp
    in_maps = []
    for c in range(n):
        m = dict(shared)
        m["xp"] = np.ascontiguousarray(inp["x_prompt"][2 * c:2 * c + 2].reshape(512, D))
        m["xs"] = np.ascontiguousarray(inp["x_sample"][c])
        cond = np.stack([inp["c_ctx"], inp["c"][c]], 0)
        m["condT"] = np.ascontiguousarray(cond.reshape(2, 16, 128).transpose(2, 1, 0))
        h0 = np.zeros((128, 2, 2, 8), np.float32)
        for l in range(2):
            h0[:, 0, l, :] = inp["state_lru_f"][c, l].reshape(8, 128).T
            h0[:, 1, l, :] = inp["state_lru_b"][c, l].reshape(8, 128).T
        m["lru_h0"] = h0.reshape(128, 32)
        m["cache_k"] = np.ascontiguousarray(inp["cache_na_k"][c].reshape(2, 256, 1024))
        m["cache_v"] = np.ascontiguousarray(inp["cache_na_v"][c].reshape(2, 256, 1024))
        m["ret_s0"] = np.ascontiguousarray(np.stack([inp["state_ret_f"][c], inp["state_ret_b"][c]], 0))
        m["ssd_s0"] = np.ascontiguousarray(np.stack([inp["state_ssd_f"][c], inp["state_ssd_b"][c]], 0))
        in_maps.append(m)
    res = run_bass_kernel_spmd(nc, in_maps, core_ids=list(range(n)))
    R = res.results
    if n < 8:
        R = list(R) + [R[0]] * (8 - n)
    n = 8
    y_prompt = np.concatenate([R[c]["yp"].reshape(2, 256, D) for c in range(n)], 0)
    y_sample = np.stack([R[c]["ys"] for c in range(n)], 0)
    nk = np.concatenate([R[c]["nk"].reshape(2, 2, 256, 8, 128) for c in range(n)], 0)
    nv = np.concatenate([R[c]["nv"].reshape(2, 2, 256, 8, 128) for c in range(n)], 0)
    lru = np.stack([R[c]["nlru"].reshape(128, 2, 2, 2, 8) for c in range(n)], 0)
    lru = lru.transpose(2, 0, 3, 4, 5, 1).reshape(2, 16, 2, 1024)
    nret = np.stack([R[c]["nret"] for c in range(n)], 0)
    nret = nret.transpose(1, 0, 2, 3, 4, 5, 6).reshape(2, 16, 2, 4, 128, 256)
    nssd = np.stack([R[c]["nssd"] for c in range(n)], 0)
    nssd = nssd.transpose(1, 0, 2, 3, 4, 5, 6).reshape(2, 16, 2, 16, 128, 64)
    f32 = lambda a: np.ascontiguousarray(a, dtype=np.float32)
    return (f32(y_prompt), f32(y_sample), f32(nk), f32(nv), f32(lru[0]), f32(lru[1]),
            f32(nret[0]), f32(nret[1]), f32(nssd[0]), f32(nssd[1]))
```
